# Optimizing a Trainium2 kernel written in Bass

```python
import jax, jax.numpy as jnp
from jax import lax
import numpy as np


D_MODEL = 1024
BATCH = 4
SEQ = 4096
DEPTH = 4

GRID_W = 64
CTX_LEN = 256
N_MIXERS = 3
EPS = 1e-6
NEG_INF = -1e30
N_HEADS = 16
N_KV_HEADS = 4
HEAD_DIM = 64
KV_GROUP = N_HEADS // N_KV_HEADS
WINDOW = 128
ATTN_BLOCK = 128
ROPE_THETA = 10000.0
QKV_WIDTH = (N_HEADS + 2 * N_KV_HEADS) * HEAD_DIM
D_RNN = D_MODEL
N_LRU_BLOCKS = 8
LRU_BLOCK = D_RNN // N_LRU_BLOCKS
LRU_CONV = 4
LRU_CONV_LEFT = 2
LRU_C = 8.0
CONF_KERNEL = 31
N_KEYS = 128
N_EXPERTS = N_KEYS * N_KEYS
PEER_HEADS = 8
PEER_QDIM = 256
PEER_TOPK = 16
PEER_CHUNK = 128

N_ATTN_LAYERS = (DEPTH + 2) // 3
N_LRU_LAYERS = (DEPTH + 1) // 3
N_CONV_LAYERS = DEPTH // 3

kernel_name = 'hybrid_diffusion_trunk'


def rms_norm(x, g):
    xf = x.astype(jnp.float32)
    y = xf * lax.rsqrt(jnp.mean(xf * xf, axis=-1, keepdims=True) + EPS)
    return (y * g.astype(jnp.float32)).astype(x.dtype)


def layer_norm(x, g, b):
    xf = x.astype(jnp.float32)
    mu = jnp.mean(xf, axis=-1, keepdims=True)
    var = jnp.mean(jnp.square(xf - mu), axis=-1, keepdims=True)
    y = (xf - mu) * lax.rsqrt(var + EPS)
    return (y * g.astype(jnp.float32) + b.astype(jnp.float32)).astype(x.dtype)


def modulate(h, shift, scale):
    return h * (1.0 + scale) + shift


def axial_rope_tables(n_tokens):
    rows = n_tokens // GRID_W
    row = jnp.repeat(jnp.arange(rows), GRID_W).astype(jnp.float32)
    col = jnp.tile(jnp.arange(GRID_W), rows).astype(jnp.float32)
    n_freq = HEAD_DIM // 4
    freqs = ROPE_THETA ** (-jnp.arange(n_freq, dtype=jnp.float32) / n_freq)
    ang = jnp.stack([row[:, None] * freqs, col[:, None] * freqs], axis=1)
    return jnp.cos(ang), jnp.sin(ang)


def apply_axial_rope(x, cos, sin):
    b, l, h, d = x.shape
    xr = x.reshape(b, l, h, 2, 2, d // 4)
    x1, x2 = xr[..., 0, :], xr[..., 1, :]
    cs = cos[None, :, None].astype(x.dtype)
    sn = sin[None, :, None].astype(x.dtype)
    out = jnp.stack([x1 * cs - x2 * sn, x2 * cs + x1 * sn], axis=-2)
    return out.reshape(b, l, h, d)


def depthwise_conv(x, w, bias, left):
    k = w.shape[0]
    ch = x.shape[-1]
    xp = jnp.pad(x, ((0, 0), (left, k - 1 - left), (0, 0)))
    y = lax.conv_general_dilated(xp, w.astype(x.dtype)[:, None, :], window_strides=(1,), padding='VALID',
                                 dimension_numbers=('NWC', 'WIO', 'NWC'), feature_group_count=ch)
    return y + bias


def windowed_gqa_mixer(hc, hl, w_qkv, q_gain, k_gain, sink, w_o, cos, sin, need_ctx):
    def project(h):
        b, l, _ = h.shape
        q, k, v = jnp.split(h @ w_qkv, [N_HEADS * HEAD_DIM, (N_HEADS + N_KV_HEADS) * HEAD_DIM], axis=-1)
        q = rms_norm(q.reshape(b, l, N_HEADS, HEAD_DIM), q_gain)
        k = rms_norm(k.reshape(b, l, N_KV_HEADS, HEAD_DIM), k_gain)
        return q, k, v.reshape(b, l, N_KV_HEADS, HEAD_DIM)

    qc, kc, vc = project(hc)
    ql, kl, vl = project(hl)
    ql = apply_axial_rope(ql, cos, sin)
    kl = apply_axial_rope(kl, cos, sin)
    b, l = hl.shape[:2]
    lc = hc.shape[1]
    scale = HEAD_DIM ** -0.5
    sink_f = sink.astype(jnp.float32).reshape(N_KV_HEADS, KV_GROUP)
    ql = ql.reshape(b, l, N_KV_HEADS, KV_GROUP, HEAD_DIM)
    pad = ((0, 0), (ATTN_BLOCK, ATTN_BLOCK), (0, 0), (0, 0))
    kl_pad = jnp.pad(kl, pad)
    vl_pad = jnp.pad(vl, pad)
    n_blocks = l // ATTN_BLOCK

    def block(bi):
        start = bi * ATTN_BLOCK
        qb = lax.dynamic_slice_in_dim(ql, start, ATTN_BLOCK, axis=1)
        kb = lax.dynamic_slice_in_dim(kl_pad, start, 3 * ATTN_BLOCK, axis=1)
        vb = lax.dynamic_slice_in_dim(vl_pad, start, 3 * ATTN_BLOCK, axis=1)
        s_loc = jnp.einsum('bqhgd,bkhd->bhgqk', qb, kb).astype(jnp.float32) * scale
        qpos = start + jnp.arange(ATTN_BLOCK)
        kpos = start - ATTN_BLOCK + jnp.arange(3 * ATTN_BLOCK)
        valid = (jnp.abs(qpos[:, None] - kpos[None, :]) <= WINDOW) & (kpos >= 0)[None, :] & (kpos < l)[None, :]
        s_loc = jnp.where(valid, s_loc, NEG_INF)
        s_ctx = jnp.einsum('bqhgd,bkhd->bhgqk', qb, kc).astype(jnp.float32) * scale
        s_sink = jnp.broadcast_to(sink_f[None, :, :, None, None], (b, N_KV_HEADS, KV_GROUP, ATTN_BLOCK, 1))
        p = jax.nn.softmax(jnp.concatenate([s_loc, s_ctx, s_sink], axis=-1), axis=-1)
        p_loc = p[..., :3 * ATTN_BLOCK].astype(vl.dtype)
        p_ctx = p[..., 3 * ATTN_BLOCK:3 * ATTN_BLOCK + lc].astype(vc.dtype)
        return (jnp.einsum('bhgqk,bkhd->bqhgd', p_loc, vb)
                + jnp.einsum('bhgqk,bkhd->bqhgd', p_ctx, vc))

    ol = lax.map(block, jnp.arange(n_blocks))
    yl = jnp.moveaxis(ol, 0, 1).reshape(b, l, N_HEADS * HEAD_DIM) @ w_o
    yc = None
    if need_ctx:
        qc = qc.reshape(b, lc, N_KV_HEADS, KV_GROUP, HEAD_DIM)
        s = jnp.einsum('bqhgd,bkhd->bhgqk', qc, kc).astype(jnp.float32) * scale
        s_sink = jnp.broadcast_to(sink_f[None, :, :, None, None], (b, N_KV_HEADS, KV_GROUP, lc, 1))
        p = jax.nn.softmax(jnp.concatenate([s, s_sink], axis=-1), axis=-1)
        oc = jnp.einsum('bhgqk,bkhd->bqhgd', p[..., :lc].astype(vc.dtype), vc)
        yc = oc.reshape(b, lc, N_HEADS * HEAD_DIM) @ w_o
    return yc, yl


def block_diag_linear(x, w, bias):
    xs = x.reshape(x.shape[:-1] + (N_LRU_BLOCKS, LRU_BLOCK))
    return jnp.einsum('...ni,nij->...nj', xs, w).reshape(x.shape) + bias


def lru_coeffs(u, w_a, b_a, w_x, b_x, lam):
    uf = u.astype(jnp.float32)
    r = jax.nn.sigmoid(block_diag_linear(uf, w_a.astype(jnp.float32), b_a.astype(jnp.float32)))
    i = jax.nn.sigmoid(block_diag_linear(uf, w_x.astype(jnp.float32), b_x.astype(jnp.float32)))
    log_a = -LRU_C * r * jax.nn.softplus(-lam.astype(jnp.float32))
    a = jnp.exp(log_a)
    bterm = jnp.sqrt(-jnp.expm1(2.0 * log_a)) * (i * uf)
    return a, bterm


def linear_scan(a, bterm, h0):
    def combine(lft, rgt):
        return lft[0] * rgt[0], rgt[0] * lft[1] + rgt[1]
    a_cum, h = lax.associative_scan(combine, (a, bterm), axis=1)
    return h + a_cum * h0[:, None, :]


def maybe_flip(t, rev):
    return t[:, ::-1] if rev else t


def rglru_mixer(hc, hl, w_in, conv_w, conv_b, w_a, b_a, w_x, b_x, lam, w_out, need_ctx):
    def branches(h):
        gate, u = jnp.split(h @ w_in, 2, axis=-1)
        return jax.nn.gelu(gate), depthwise_conv(u, conv_w, conv_b, LRU_CONV_LEFT)

    gate_c, u_c = branches(hc)
    gate_l, u_l = branches(hl)
    h0 = jnp.zeros((hc.shape[0], D_RNN), jnp.float32)
    dirs_c, dirs_l = [], []
    for d in range(2):
        rev = d == 1
        a_c, b_c = lru_coeffs(maybe_flip(u_c, rev), w_a[d], b_a[d], w_x[d], b_x[d], lam[d])
        h_c = linear_scan(a_c, b_c, h0)
        a_l, b_l = lru_coeffs(maybe_flip(u_l, rev), w_a[d], b_a[d], w_x[d], b_x[d], lam[d])
        h_l = linear_scan(a_l, b_l, h_c[:, -1])
        dirs_c.append(maybe_flip(h_c, rev))
        dirs_l.append(maybe_flip(h_l, rev))
    yl = ((dirs_l[0] + dirs_l[1]).astype(hl.dtype) * gate_l) @ w_out
    yc = ((dirs_c[0] + dirs_c[1]).astype(hc.dtype) * gate_c) @ w_out if need_ctx else None
    return yc, yl


def conformer_conv_mixer(hc, hl, w_pw1, b_pw1, dw_w, dw_b, ln_g, ln_b, w_pw2, b_pw2, need_ctx):
    def conv_module(h):
        val, gate = jnp.split(h @ w_pw1 + b_pw1, 2, axis=-1)
        u = depthwise_conv(val * jax.nn.sigmoid(gate), dw_w, dw_b, CONF_KERNEL // 2)
        u = jax.nn.silu(layer_norm(u, ln_g, ln_b))
        return u @ w_pw2 + b_pw2
    yc = conv_module(hc) if need_ctx else None
    return yc, conv_module(hl)


def peer_ffn(h, w_q, keys1, keys2, u_tab, v_tab):
    b, l, d = h.shape
    q = (h @ w_q).reshape(b, l, PEER_HEADS, 2, PEER_QDIM // 2)
    s1 = jnp.einsum('blhd,hkd->blhk', q[..., 0, :], keys1).astype(jnp.float32)
    s2 = jnp.einsum('blhd,hkd->blhk', q[..., 1, :], keys2).astype(jnp.float32)
    v1, i1 = lax.top_k(s1, PEER_TOPK)
    v2, i2 = lax.top_k(s2, PEER_TOPK)
    n_cand = PEER_TOPK * PEER_TOPK
    cand = (v1[..., :, None] + v2[..., None, :]).reshape(b, l, PEER_HEADS, n_cand)
    cand_idx = (i1[..., :, None] * N_KEYS + i2[..., None, :]).reshape(b, l, PEER_HEADS, n_cand)
    top_s, pos = lax.top_k(cand, PEER_TOPK)
    expert = jnp.take_along_axis(cand_idx, pos, axis=-1)
    gate = jax.nn.softmax(top_s, axis=-1).astype(h.dtype)
    n_chunks = (b * l) // PEER_CHUNK
    hx = h.reshape(n_chunks, PEER_CHUNK, d)
    ex = expert.reshape(n_chunks, PEER_CHUNK, PEER_HEADS, PEER_TOPK)
    gx = gate.reshape(n_chunks, PEER_CHUNK, PEER_HEADS, PEER_TOPK)

    def chunk(args):
        xt, e, g = args
        u = u_tab[e]
        v = v_tab[e]
        act = jax.nn.gelu(jnp.einsum('cd,chkd->chk', xt, u))
        return jnp.einsum('chk,chkd->cd', g * act, v)

    return lax.map(chunk, (hx, ex, gx)).reshape(b, l, d)


def setup_inputs(seed: int = 0) -> dict:
    key = jax.random.key(seed)
    ks = iter(jax.random.split(key, 48))
    f32 = jnp.float32

    def nrm(shape, scale):
        return jax.random.normal(next(ks), shape, dtype=f32) * scale

    def gain(shape):
        return 1.0 + nrm(shape, 0.02)

    u_lam = jax.random.uniform(next(ks), (N_LRU_LAYERS, 2, D_RNN), dtype=f32, minval=0.9, maxval=0.999)
    a_lam = u_lam ** (1.0 / LRU_C)
    lam = jnp.log(a_lam) - jnp.log1p(-a_lam)
    return {
        'x': nrm((BATCH, SEQ, D_MODEL), 1.0),
        'c': nrm((BATCH, D_MODEL), 1.0),
        'ctx': nrm((BATCH, CTX_LEN, D_MODEL), 1.0),
        'c_ctx': nrm((D_MODEL,), 1.0),
        'ada_w': nrm((DEPTH, D_MODEL, 6 * D_MODEL), 0.5 * D_MODEL ** -0.5),
        'ada_b': nrm((DEPTH, 6 * D_MODEL), 0.01),
        'norm_g': gain((DEPTH, 2, D_MODEL)),
        'attn_wqkv': nrm((N_ATTN_LAYERS, D_MODEL, QKV_WIDTH), D_MODEL ** -0.5),
        'attn_q_gain': gain((N_ATTN_LAYERS, HEAD_DIM)),
        'attn_k_gain': gain((N_ATTN_LAYERS, HEAD_DIM)),
        'attn_sink': nrm((N_ATTN_LAYERS, N_HEADS), 0.5),
        'attn_wo': nrm((N_ATTN_LAYERS, N_HEADS * HEAD_DIM, D_MODEL), (N_HEADS * HEAD_DIM) ** -0.5),
        'lru_w_in': nrm((N_LRU_LAYERS, D_MODEL, 2 * D_RNN), D_MODEL ** -0.5),
        'lru_conv_w': nrm((N_LRU_LAYERS, LRU_CONV, D_RNN), LRU_CONV ** -0.5),
        'lru_conv_b': nrm((N_LRU_LAYERS, D_RNN), 0.01),
        'lru_w_a': nrm((N_LRU_LAYERS, 2, N_LRU_BLOCKS, LRU_BLOCK, LRU_BLOCK), LRU_BLOCK ** -0.5),
        'lru_b_a': nrm((N_LRU_LAYERS, 2, D_RNN), 0.01),
        'lru_w_x': nrm((N_LRU_LAYERS, 2, N_LRU_BLOCKS, LRU_BLOCK, LRU_BLOCK), LRU_BLOCK ** -0.5),
        'lru_b_x': nrm((N_LRU_LAYERS, 2, D_RNN), 0.01),
        'lru_lambda': lam,
        'lru_w_out': nrm((N_LRU_LAYERS, D_RNN, D_MODEL), D_RNN ** -0.5),
        'conf_w_pw1': nrm((N_CONV_LAYERS, D_MODEL, 2 * D_MODEL), D_MODEL ** -0.5),
        'conf_b_pw1': nrm((N_CONV_LAYERS, 2 * D_MODEL), 0.01),
        'conf_dw_w': nrm((N_CONV_LAYERS, CONF_KERNEL, D_MODEL), CONF_KERNEL ** -0.5),
        'conf_dw_b': nrm((N_CONV_LAYERS, D_MODEL), 0.01),
        'conf_ln_g': gain((N_CONV_LAYERS, D_MODEL)),
        'conf_ln_b': nrm((N_CONV_LAYERS, D_MODEL), 0.01),
        'conf_w_pw2': nrm((N_CONV_LAYERS, D_MODEL, D_MODEL), D_MODEL ** -0.5),
        'conf_b_pw2': nrm((N_CONV_LAYERS, D_MODEL), 0.01),
        'peer_wq': nrm((DEPTH, D_MODEL, PEER_HEADS * PEER_QDIM), D_MODEL ** -0.5),
        'peer_keys1': nrm((DEPTH, PEER_HEADS, N_KEYS, PEER_QDIM // 2), (PEER_QDIM // 2) ** -0.5),
        'peer_keys2': nrm((DEPTH, PEER_HEADS, N_KEYS, PEER_QDIM // 2), (PEER_QDIM // 2) ** -0.5),
        'peer_u': nrm((DEPTH, N_EXPERTS, D_MODEL), D_MODEL ** -0.5),
        'peer_v': nrm((DEPTH, N_EXPERTS, D_MODEL), PEER_HEADS ** -0.5),
    }


def reference(x, c, ctx, c_ctx, ada_w, ada_b, norm_g, attn_wqkv, attn_q_gain, attn_k_gain, attn_sink, attn_wo,
              lru_w_in, lru_conv_w, lru_conv_b, lru_w_a, lru_b_a, lru_w_x, lru_b_x, lru_lambda, lru_w_out,
              conf_w_pw1, conf_b_pw1, conf_dw_w, conf_dw_b, conf_ln_g, conf_ln_b, conf_w_pw2, conf_b_pw2,
              peer_wq, peer_keys1, peer_keys2, peer_u, peer_v):
    cos, sin = axial_rope_tables(x.shape[1])
    silu_c = jax.nn.silu(c)
    silu_cc = jax.nn.silu(c_ctx)
    xl, xc = x, ctx
    for layer in range(DEPTH):
        need_ctx = layer < DEPTH - 1
        kind = layer % N_MIXERS
        slot = layer // N_MIXERS
        mod_l = (silu_c @ ada_w[layer] + ada_b[layer])[:, None, :]
        mod_c = silu_cc @ ada_w[layer] + ada_b[layer]
        sh1l, sc1l, g1l, sh2l, sc2l, g2l = jnp.split(mod_l, 6, axis=-1)
        sh1c, sc1c, g1c, sh2c, sc2c, g2c = jnp.split(mod_c, 6, axis=-1)
        hl = modulate(rms_norm(xl, norm_g[layer, 0]), sh1l, sc1l)
        hc = modulate(rms_norm(xc, norm_g[layer, 0]), sh1c, sc1c)
        if kind == 0:
            yc, yl = windowed_gqa_mixer(hc, hl, attn_wqkv[slot], attn_q_gain[slot], attn_k_gain[slot],
                                        attn_sink[slot], attn_wo[slot], cos, sin, need_ctx)
        elif kind == 1:
            yc, yl = rglru_mixer(hc, hl, lru_w_in[slot], lru_conv_w[slot], lru_conv_b[slot], lru_w_a[slot],
                                 lru_b_a[slot], lru_w_x[slot], lru_b_x[slot], lru_lambda[slot], lru_w_out[slot],
                                 need_ctx)
        else:
            yc, yl = conformer_conv_mixer(hc, hl, conf_w_pw1[slot], conf_b_pw1[slot], conf_dw_w[slot],
                                          conf_dw_b[slot], conf_ln_g[slot], conf_ln_b[slot], conf_w_pw2[slot],
                                          conf_b_pw2[slot], need_ctx)
        xl = xl + g1l * yl
        hl2 = modulate(rms_norm(xl, norm_g[layer, 1]), sh2l, sc2l)
        xl = xl + g2l * peer_ffn(hl2, peer_wq[layer], peer_keys1[layer], peer_keys2[layer], peer_u[layer], peer_v[layer])
        if need_ctx:
            xc = xc + g1c * yc
            hc2 = modulate(rms_norm(xc, norm_g[layer, 1]), sh2c, sc2c)
            xc = xc + g2c * peer_ffn(hc2, peer_wq[layer], peer_keys1[layer], peer_keys2[layer], peer_u[layer], peer_v[layer])
    return xl
```

```python
import numpy as np
import concourse.bass as bass
import concourse.mybir as mybir
from concourse.bass_utils import run_bass_kernel_spmd
from contextlib import ExitStack

F32 = mybir.dt.float32
BF16 = mybir.dt.bfloat16
U32 = mybir.dt.uint32
AF = mybir.ActivationFunctionType
ALU = mybir.AluOpType
AX = mybir.AxisListType

NCORES = 8
D = 1024
DC = 8
B = 4
L = 4096
LC = 256
TOK = 2176
EPS = 1e-6


class Sched:
    ENG = ("pe", "dve", "act", "pool", "sp")

    def __init__(self, nc, es, n_dma_sems=32):
        self.nc = nc
        self.es = es
        self.e = dict(pe=nc.tensor, dve=nc.vector, act=nc.scalar, pool=nc.gpsimd, sp=nc.sync)
        self.sem = {k: es.enter_context(nc.semaphore("s_" + k)) for k in self.ENG}
        self.cnt = {k: 0 for k in self.ENG}
        self.dsem = [es.enter_context(nc.semaphore("d%d" % i)) for i in range(n_dma_sems)]
        self.dcnt = [0] * n_dma_sems
        self.dnext = 0
        self.waited = {}
        self.lastw = {}
        self.readers = {}
        self.ntiles = 0

    def sb(self, name, shape, dtype=F32):
        return self.es.enter_context(self.nc.sbuf_tensor("t_" + name, list(shape), dtype))

    def ps(self, name, shape, dtype=F32):
        return self.es.enter_context(self.nc.psum_tensor("p_" + name, list(shape), dtype))

    def _wait(self, eng, dep):
        semkey, val = dep
        if semkey == ("e", "pe") and eng == "pe":
            return
        k = (eng, semkey)
        if self.waited.get(k, 0) >= val:
            return
        self.waited[k] = val
        sem = self.sem[semkey[1]] if semkey[0] == "e" else self.dsem[semkey[1]]
        self.e[eng].wait_ge(sem, val)

    def _deps(self, r, w):
        deps = []
        for k in r:
            if k in self.lastw:
                deps.append(self.lastw[k])
        for k in w:
            if k in self.lastw:
                deps.append(self.lastw[k])
            deps.extend(self.readers.get(k, {}).items())
        return deps

    def _record(self, tok, r, w):
        for k in r:
            d = self.readers.setdefault(k, {})
            if d.get(tok[0], 0) < tok[1]:
                d[tok[0]] = tok[1]
        for k in w:
            self.lastw[k] = tok
            self.readers[k] = {}

    def I(self, eng, fn, r=(), w=(), sig=True):
        for d in self._deps(r, w):
            self._wait(eng, d)
        ins = fn(self.e[eng])
        if sig:
            self.cnt[eng] += 1
            ins.then_inc(self.sem[eng], 1)
            tok = (("e", eng), self.cnt[eng])
        else:
            tok = (("e", eng), self.cnt[eng] + 1)
        self._record(tok, r, w)
        return ins

    def dma(self, q, out, in_, r=(), w=()):
        if q == "pool":
            q = "sp"
        i = self.dnext
        self.dnext = (self.dnext + 1) % len(self.dsem)
        if self.dcnt[i] > 0:
            self._wait(q, (("d", i), self.dcnt[i]))
        for d in self._deps(r, w):
            self._wait(q, d)
        ins = self.e[q].dma_start(out=out, in_=in_)
        self.dcnt[i] += 16
        ins.then_inc(self.dsem[i], 16)
        self._record((("d", i), self.dcnt[i]), r, w)
        return ins

    def finish(self):
        for i, c in enumerate(self.dcnt):
            if c > 0:
                self._wait("sp", (("d", i), c))
        for k in self.ENG:
            if k != "sp" and self.cnt[k] > 0:
                self._wait("sp", (("e", k), self.cnt[k]))


def AP(t, offset, dims, nparts=128):
    base = t[:].ap[0][0]
    return bass.AP(tensor=t, offset=offset, ap=[[base, nparts]] + [list(d) for d in dims])


def new_nc():
    return bass.Bass("TRN2", target_bir_lowering=False)


def din(nc, name, shape, dtype=F32):
    return nc.dram_tensor(name, list(shape), dtype, kind="ExternalInput").ap()


def dout(nc, name, shape, dtype=F32):
    return nc.dram_tensor(name, list(shape), dtype, kind="ExternalOutput").ap()


def run(nc, in_maps):
    res = run_bass_kernel_spmd(nc, in_maps, core_ids=list(range(len(in_maps))))
    return res.results


def build_ada():
    nc = new_nc()
    NH = 3072
    w = din(nc, "w", [D, NH])
    cv = din(nc, "cv", [128, DC, 5])
    bb = din(nc, "b", [128, NH // 128])
    o = dout(nc, "o", [128, NH // 128, 5])
    with ExitStack() as es:
        s = Sched(nc, es)
        wt = s.sb("wt", [128, DC, NH])
        cvt = s.sb("cvt", [128, DC, 5])
        sv = s.sb("sv", [128, DC, 5])
        bt = s.sb("bt", [128, NH // 128])
        ot = s.sb("ot", [128, NH // 128, 5])
        pp = [s.ps("pp%d" % i, [128, 512]) for i in range(2)]
        for k in range(DC):
            s.dma("sp" if k % 2 == 0 else "pool", wt[:, k, :], w[k * 128:(k + 1) * 128, :], w=["wt%d" % k])
        s.dma("sp", cvt[:], cv, w=["cvt"])
        s.dma("sp", bt[:], bb, w=["bt"])
        s.I("act", lambda e: e.activation(out=sv[:], in_=cvt[:], func=AF.Silu), r=["cvt"], w=["sv"])
        for j in range(NH // 128):
            p = pp[j % 2]
            pk = "pp%d" % (j % 2)
            for k in range(DC):
                s.I("pe", lambda e: e.matmul(p[:, 0:5], lhsT=wt[:, k, j * 128:(j + 1) * 128], rhs=sv[:, k, :],
                                             start=(k == 0), stop=(k == DC - 1)),
                    r=["wt%d" % k, "sv"], w=[pk], sig=(k == DC - 1))
            s.I("dve", lambda e: e.tensor_scalar(out=ot[:, j, :], in0=p[:, 0:5], scalar1=bt[:, j:j + 1], scalar2=None,
                                                 op0=ALU.add), r=[pk, "bt"], w=["ot"])
        s.dma("sp", o, ot[:], r=["ot"])
        s.finish()
    return nc


def run_ada(inp):
    nc = build_ada()
    cvec = np.concatenate([inp["c"], inp["c_ctx"][None, :]], 0)
    cv = np.ascontiguousarray(cvec.T.reshape(DC, 128, 5).transpose(1, 0, 2))
    maps = []
    for c in range(NCORES):
        l, h = c // 2, c % 2
        w = np.ascontiguousarray(inp["ada_w"][l][:, h * 3072:(h + 1) * 3072])
        b = np.ascontiguousarray(inp["ada_b"][l][h * 3072:(h + 1) * 3072].reshape(24, 128).T)
        maps.append({"w": w, "cv": cv, "b": b})
    res = run(nc, maps)
    mod = np.zeros((4, 5, 6144), np.float32)
    for c in range(NCORES):
        l, h = c // 2, c % 2
        o = res[c]["o"]
        mod[l, :, h * 3072:(h + 1) * 3072] = o.transpose(2, 1, 0).reshape(5, 3072)
    return mod


TILES = [(0, 128, 1), (128, 512, 0), (640, 512, 0), (1152, 512, 0), (1664, 512, 0)]


def emit_modprep(s, mods, ab):
    for i, src in enumerate((1, 3)):
        s.I("dve", lambda e: e.scalar_tensor_tensor(out=ab[:, i, :], in0=mods[:, src, :], scalar=1.0, in1=mods[:, 0, :],
                                                    op0=ALU.add, op1=ALU.mult), r=["mods"], w=["ab"])


def emit_normmod(s, xt, xkey, n, is_ctx, ht, hkey, sq, rstd, ones, ab, mods, pss, hb=None, hbkey=None):
    s.I("act", lambda e: e.activation(out=sq[:, :, :n], in_=xt[:, :, :n], func=AF.Square), r=[xkey], w=["sq"])
    for k in range(DC):
        s.I("pe", lambda e: e.matmul(pss[:, :n], lhsT=ones[:], rhs=sq[:, k, :n], start=(k == 0), stop=(k == DC - 1)),
            r=["sq", "ones"], w=["pss"], sig=(k == DC - 1))
    s.I("act", lambda e: e.activation(out=rstd[:, :n], in_=pss[:, :n], func=AF.Sqrt, scale=1.0 / D, bias=s.epsb[:]),
        r=["pss"], w=["rstd"])
    s.I("dve", lambda e: e.reciprocal(out=rstd[:, :n], in_=rstd[:, :n]), r=["rstd"], w=["rstd"])
    ai = 1 if is_ctx else 0
    shi = 4 if is_ctx else 2
    for k in range(DC):
        s.I("dve", lambda e: e.tensor_tensor(out=sq[:, k, :n], in0=xt[:, k, :n], in1=rstd[:, :n], op=ALU.mult),
            r=[xkey, "rstd"], w=["sq"])
        s.I("act", lambda e: e.activation(out=ht[:, k, :n], in_=sq[:, k, :n], func=AF.Identity,
                                          scale=ab[:, ai, k:k + 1], bias=mods[:, shi, k:k + 1]),
            r=["sq", "ab", "mods"], w=[hkey])
        if hb is not None:
            s.I("pool", lambda e: e.tensor_copy(out=hb[:, k, :n], in_=ht[:, k, :n]), r=[hkey], w=[hbkey])


def emit_consts(s):
    s.epsb = s.sb("epsb", [128, 1])
    s.I("dve", lambda e: e.memset(s.epsb[:], EPS), w=["epsb"])


def build_pre(N):
    nc = new_nc()
    NJ = N // 128
    x = din(nc, "x", [D, TOK])
    w = din(nc, "w", [D, N])
    mods_d = din(nc, "mods", [128, 5, 8])
    bias_d = din(nc, "bias", [128, NJ])
    ones_d = din(nc, "ones", [128, 128])
    y = dout(nc, "y", [N, TOK])
    xr = x.rearrange("(k p) t -> p k t", p=128)
    with ExitStack() as es:
        s = Sched(nc, es)
        emit_consts(s)
        wt = s.sb("wt", [128, DC, N])
        mods = s.sb("mods", [128, 5, 8])
        ab = s.sb("ab", [128, 2, 8])
        bias = s.sb("bias", [128, NJ])
        ones = s.sb("ones", [128, 128])
        xts = [s.sb("xt%d" % i, [128, DC, 512]) for i in range(2)]
        hts = [s.sb("ht%d" % i, [128, DC, 512]) for i in range(2)]
        sq = s.sb("sq", [128, DC, 512])
        rstd = s.sb("rstd", [128, 512])
        ots = [s.sb("ot%d" % i, [128, 512]) for i in range(4)]
        pss = s.ps("pss", [128, 512])
        pmm = [s.ps("pmm%d" % i, [128, 512]) for i in range(3)]
        s.dma("sp", mods[:], mods_d, w=["mods"])
        s.dma("sp", bias[:], bias_d, w=["bias"])
        s.dma("sp", ones[:], ones_d, w=["ones"])
        for k in range(DC):
            s.dma("pool" if k % 2 else "sp", wt[:, k, :], w[k * 128:(k + 1) * 128, :], w=["wt%d" % k])
        emit_modprep(s, mods, ab)
        oc = 0
        for ti, (c0, n, isc) in enumerate(TILES):
            xt, xk = xts[ti % 2], "xt%d" % (ti % 2)
            ht, hk = hts[ti % 2], "ht%d" % (ti % 2)
            s.dma("sp", xt[:, :, :n], xr[:, :, c0:c0 + n], w=[xk])
            emit_normmod(s, xt, xk, n, isc, ht, hk, sq, rstd, ones, ab, mods, pss)
            for j in range(NJ):
                p, pk = pmm[j % 3], "pmm%d" % (j % 3)
                for k in range(DC):
                    s.I("pe", lambda e: e.matmul(p[:, :n], lhsT=wt[:, k, j * 128:(j + 1) * 128], rhs=ht[:, k, :n],
                                                 start=(k == 0), stop=(k == DC - 1)),
                        r=["wt%d" % k, hk], w=[pk], sig=(k == DC - 1))
                ot, ok = ots[oc % 4], "ot%d" % (oc % 4)
                oc += 1
                if j % 2 == 0:
                    s.I("act", lambda e: e.activation(out=ot[:, :n], in_=p[:, :n], func=AF.Identity,
                                                      bias=bias[:, j:j + 1]), r=[pk, "bias"], w=[ok])
                else:
                    s.I("dve", lambda e: e.tensor_scalar(out=ot[:, :n], in0=p[:, :n], scalar1=bias[:, j:j + 1],
                                                         scalar2=None, op0=ALU.add), r=[pk, "bias"], w=[ok])
                s.dma("sp", y[j * 128:(j + 1) * 128, c0:c0 + n], ot[:, :n], r=[ok])
        s.finish()
    return nc


def pm(v):
    return np.ascontiguousarray(np.asarray(v, np.float32).reshape(-1, 128).T)


def make_mods(mod, norm_g, layer, b, which):
    ml = mod[layer, b].reshape(6, D)
    mc = mod[layer, 4].reshape(6, D)
    sh, sc = (0, 1) if which == 0 else (3, 4)
    return np.ascontiguousarray(np.stack([pm(norm_g[layer, which]), pm(ml[sc]), pm(ml[sh]), pm(mc[sc]), pm(mc[sh])], 1))


TILES2 = [(0, 128, 1)] + [(128 + 256 * i, 256, 0) for i in range(8)]
NEG = -1.0e30


def build_c1():
    nc = new_nc()
    x = din(nc, "x", [D, TOK])
    yin = din(nc, "yin", [D, TOK])
    wout_d = din(nc, "wout", [D, D])
    bout_d = din(nc, "bout", [128, 8])
    mods_d = din(nc, "mods", [128, 5, 8])
    gates_d = din(nc, "gates", [128, 2, 8])
    wq_d = din(nc, "wq", [D, 2048])
    keys_d = din(nc, "keys", [128, 16, 128])
    ones_d = din(nc, "ones", [128, 128])
    x1_d = dout(nc, "x1", [D, TOK])
    h2b_d = dout(nc, "h2b", [D, TOK], BF16)
    sc_d = dout(nc, "sc", [TOK, 2048])
    tc_d = dout(nc, "tc", [TOK, 16])
    xr = x.rearrange("(k p) t -> p k t", p=128)
    yr = yin.rearrange("(k p) t -> p k t", p=128)
    x1r = x1_d.rearrange("(k p) t -> p k t", p=128)
    h2r = h2b_d.rearrange("(k p) t -> p k t", p=128)
    with ExitStack() as es:
        s = Sched(nc, es)
        emit_consts(s)
        wout = s.sb("wout", [128, DC, D])
        wq = s.sb("wq", [128, DC, 2048])
        keys = s.sb("keys", [128, 16, 128])
        mods = s.sb("mods", [128, 5, 8])
        ab = s.sb("ab", [128, 2, 8])
        gates = s.sb("gates", [128, 2, 8])
        bout = s.sb("bout", [128, 8])
        ones = s.sb("ones", [128, 128])
        xt = s.sb("xt", [128, DC, 256])
        yt = s.sb("yt", [128, DC, 256])
        x1t = s.sb("x1t", [128, DC, 256])
        sq = s.sb("sq", [128, DC, 256])
        ht = s.sb("ht", [128, DC, 256])
        hb = s.sb("hb", [128, DC, 256], BF16)
        rstd = s.sb("rstd", [128, 256])
        qT = s.sb("qT", [128, 16, 256])
        ssb = s.sb("ssb", [128, 2048])
        stmp = s.sb("stmp", [128, 2048])
        v16 = s.sb("v16", [128, 16, 16])
        cand = s.sb("cand", [128, 8, 256])
        cand2 = s.sb("cand2", [128, 8, 256])
        c16 = s.sb("c16", [128, 8, 16])
        e16 = s.sb("e16", [128, 8, 16])
        zz = s.sb("zz", [128, 8])
        tcs = s.sb("tcs", [128, 16])
        pss = s.ps("pss", [128, 512])
        pmm = [s.ps("pmm%d" % i, [128, 512]) for i in range(2)]
        psc = s.ps("psc", [128, 2048])
        for nm, t, d in (("mods", mods, mods_d), ("gates", gates, gates_d), ("bout", bout, bout_d),
                         ("ones", ones, ones_d), ("keys", keys, keys_d)):
            s.dma("sp", t[:], d, w=[nm])
        for k in range(DC):
            s.dma("pool" if k % 2 else "sp", wout[:, k, :], wout_d[k * 128:(k + 1) * 128, :], w=["wout%d" % k])
        for k in range(DC):
            s.dma("pool" if k % 2 else "sp", wq[:, k, :], wq_d[k * 128:(k + 1) * 128, :], w=["wq%d" % k])
        emit_modprep(s, mods, ab)
        for ti, (c0, n, isc) in enumerate(TILES2):
            gi = 1 if isc else 0
            s.dma("sp", xt[:, :, :n], xr[:, :, c0:c0 + n], w=["xt"])
            s.dma("sp", yt[:, :, :n], yr[:, :, c0:c0 + n], w=["yt"])
            for j in range(DC):
                p, pk = pmm[j % 2], "pmm%d" % (j % 2)
                for k in range(DC):
                    s.I("pe", lambda e: e.matmul(p[:, :n], lhsT=wout[:, k, j * 128:(j + 1) * 128], rhs=yt[:, k, :n],
                                                 start=(k == 0), stop=(k == DC - 1)),
                        r=["wout%d" % k, "yt"], w=[pk], sig=(k == DC - 1))
                s.I("act", lambda e: e.activation(out=x1t[:, j, :n], in_=p[:, :n], func=AF.Identity,
                                                  bias=bout[:, j:j + 1]), r=[pk, "bout"], w=["x1t"])
                s.I("dve", lambda e: e.scalar_tensor_tensor(out=x1t[:, j, :n], in0=x1t[:, j, :n],
                                                            scalar=gates[:, gi, j:j + 1], in1=xt[:, j, :n],
                                                            op0=ALU.mult, op1=ALU.add),
                    r=["x1t", "gates", "xt"], w=["x1t"])
            s.dma("sp", x1r[:, :, c0:c0 + n], x1t[:, :, :n], r=["x1t"])
            emit_normmod(s, x1t, "x1t", n, isc, ht, "ht", sq, rstd, ones, ab, mods, pss, hb, "hb")
            s.dma("sp", h2r[:, :, c0:c0 + n], hb[:, :, :n], r=["hb"])
            for c in range(16):
                p, pk = pmm[c % 2], "pmm%d" % (c % 2)
                for k in range(DC):
                    s.I("pe", lambda e: e.matmul(p[:, :n], lhsT=wq[:, k, c * 128:(c + 1) * 128], rhs=ht[:, k, :n],
                                                 start=(k == 0), stop=(k == DC - 1)),
                        r=["wq%d" % k, "ht"], w=[pk], sig=(k == DC - 1))
                if c % 2 == 0:
                    s.I("act", lambda e: e.activation(out=qT[:, c, :n], in_=p[:, :n], func=AF.Copy), r=[pk], w=["qT"])
                else:
                    s.I("dve", lambda e: e.tensor_copy(out=qT[:, c, :n], in_=p[:, :n]), r=[pk], w=["qT"])
            for st in range(n // 128):
                t0 = c0 + st * 128
                for c in range(16):
                    s.I("pe", lambda e: e.matmul(psc[:, c * 128:(c + 1) * 128], lhsT=qT[:, c, st * 128:(st + 1) * 128],
                                                 rhs=keys[:, c, :], start=True, stop=True),
                        r=["qT", "keys"], w=["psc"], sig=(c == 15))
                s.I("act", lambda e: e.activation(out=ssb[:], in_=psc[:], func=AF.Copy), r=["psc"], w=["ssb"])
                s.dma("sp", sc_d[t0:t0 + 128, :], ssb[:], r=["ssb"])
                for g in range(16):
                    sl = slice(g * 128, (g + 1) * 128)
                    s.I("dve", lambda e: e.max(out=v16[:, g, 0:8], in_=ssb[:, sl]), r=["ssb"], w=["v16"])
                    s.I("dve", lambda e: e.match_replace(out=stmp[:, sl], in_to_replace=v16[:, g, 0:8],
                                                         in_values=ssb[:, sl], imm_value=NEG),
                        r=["ssb", "v16"], w=["stmp"])
                    s.I("dve", lambda e: e.max(out=v16[:, g, 8:16], in_=stmp[:, sl]), r=["stmp"], w=["v16"])
                s.I("pool", lambda e: e.tensor_tensor(out=AP(cand, 0, [[256, 8], [16, 16], [1, 16]]),
                                                      in0=AP(v16, 0, [[32, 8], [1, 16], [0, 16]]),
                                                      in1=AP(v16, 16, [[32, 8], [0, 16], [1, 16]]), op=ALU.add),
                    r=["v16"], w=["cand"])
                for h in range(8):
                    s.I("dve", lambda e: e.max(out=c16[:, h, 0:8], in_=cand[:, h, :]), r=["cand"], w=["c16"])
                    s.I("dve", lambda e: e.match_replace(out=cand2[:, h, :], in_to_replace=c16[:, h, 0:8],
                                                         in_values=cand[:, h, :], imm_value=NEG),
                        r=["cand", "c16"], w=["cand2"])
                    s.I("dve", lambda e: e.max(out=c16[:, h, 8:16], in_=cand2[:, h, :]), r=["cand2"], w=["c16"])
                s.I("dve", lambda e: e.tensor_tensor(out=e16[:], in0=c16[:], in1=AP(c16, 0, [[16, 8], [0, 16]]),
                                                     op=ALU.subtract), r=["c16"], w=["e16"])
                s.I("act", lambda e: e.activation(out=e16[:], in_=e16[:], func=AF.Exp), r=["e16"], w=["e16"])
                s.I("dve", lambda e: e.tensor_reduce(out=zz[:], in_=e16[:], axis=AX.X, op=ALU.add), r=["e16"], w=["zz"])
                s.I("act", lambda e: e.activation(out=zz[:], in_=zz[:], func=AF.Ln), r=["zz"], w=["zz"])
                s.I("dve", lambda e: e.tensor_tensor(out=tcs[:, 8:16], in0=zz[:], in1=AP(c16, 0, [[16, 8]]), op=ALU.add),
                    r=["zz", "c16"], w=["tcs"])
                s.I("dve", lambda e: e.tensor_copy(out=tcs[:, 0:8], in_=AP(c16, 15, [[16, 8]])), r=["c16"], w=["tcs"])
                s.dma("sp", tc_d[t0:t0 + 128, :], tcs[:], r=["tcs"])
        s.finish()
    return nc


def build_wcast():
    nc = new_nc()
    uf = din(nc, "uf", [16, 128, 1024])
    vf = din(nc, "vf", [16, 128, 1024])
    ub = dout(nc, "ub", [16, 128, 1024], BF16)
    vb = dout(nc, "vb", [16, 128, 1024], BF16)
    with ExitStack() as es:
        s = Sched(nc, es)
        fin = [s.sb("fin%d" % i, [128, 4, 1024]) for i in range(3)]
        fo = [s.sb("fo%d" % i, [128, 4, 1024], BF16) for i in range(3)]
        engs = ["dve", "act", "pool"]
        i = 0
        for src, dst in ((uf, ub), (vf, vb)):
            for g in range(4):
                a, ak = fin[i % 3], "fin%d" % (i % 3)
                o, ok = fo[i % 3], "fo%d" % (i % 3)
                s.dma("sp", a[:], src[g * 4:(g + 1) * 4].rearrange("c p f -> p c f"), w=[ak])
                en = engs[i % 3]
                if en == "act":
                    s.I("act", lambda e: e.activation(out=o[:], in_=a[:], func=AF.Copy), r=[ak], w=[ok])
                else:
                    s.I(en, lambda e: e.tensor_copy(out=o[:], in_=a[:]), r=[ak], w=[ok])
                s.dma("sp", dst[g * 4:(g + 1) * 4].rearrange("c p f -> p c f"), o[:], r=[ok])
                i += 1
        s.finish()
    return nc


def build_c2(gelu_func=None, NQ=8, tiles=None, do_g=True, do_e=True):
    nc = new_nc()
    gelu_func = gelu_func or AF.Gelu_apprx_tanh
    x1_d = din(nc, "x1", [D, TOK])
    h2b_d = din(nc, "h2b", [D, TOK], BF16)
    sc_d = din(nc, "sc", [TOK, 2048])
    tc_d = din(nc, "tc", [TOK, 16])
    g2_d = din(nc, "g2", [128, 2, 8])
    ut_d = din(nc, "ut", [128, 128, 1024], BF16)
    vb_d = din(nc, "vb", [128, 128, 1024], BF16)
    identb_d = din(nc, "identb", [128, 128], BF16)
    identf_d = din(nc, "identf", [128, 128])
    x2_d = dout(nc, "x2", [D, TOK])
    x1r = x1_d.rearrange("(k p) t -> p k t", p=128)
    h2r = h2b_d.rearrange("(k p) t -> p k t", p=128)
    x2r = x2_d.rearrange("(k p) t -> p k t", p=128)
    IQ = 128 // NQ
    with ExitStack() as es:
        s = Sched(nc, es)
        g2 = s.sb("g2", [128, 2, 8])
        identb = s.sb("identb", [128, 128], BF16)
        identf = s.sb("identf", [128, 128])
        st_ = s.sb("st", [128, 2048])
        tcs = s.sb("tcs", [128, 16])
        negc = s.sb("negc", [128, 8])
        Sb = [s.sb("S%d" % i, [128, IQ * 128]) for i in range(2)]
        Eb = [s.sb("E%d" % i, [128, IQ * 128], BF16) for i in range(2)]
        Gh = s.sb("Gh", [128, IQ * 128], BF16)
        Gb = [s.sb("G%d" % i, [128, IQ * 128], BF16) for i in range(2)]
        GT = s.sb("GT", [128, 128, 256], BF16)
        UG = [s.sb("UG%d" % i, [128, 4, 1024], BF16) for i in range(2)]
        VG = [s.sb("VG%d" % i, [128, 4, 1024], BF16) for i in range(2)]
        hbt = s.sb("hbt", [128, DC, 256], BF16)
        x1t = s.sb("x1t", [128, DC, 256])
        x2t = s.sb("x2t", [128, DC, 256])
        geb = [s.sb("ge%d" % i, [128, 256], BF16) for i in range(2)]
        ptb = [s.sb("pt%d" % i, [128, 256], BF16) for i in range(2)]
        osb = s.sb("osb", [128, 1024])
        pact = [s.ps("pa%d" % i, [128, 512]) for i in range(2)]
        po = [s.ps("po%d" % i, [128, 512]) for i in range(4)]
        pT = s.ps("pT", [128, 1024], BF16)
        for nm, t, d in (("g2", g2, g2_d), ("identb", identb, identb_d), ("identf", identf, identf_d)):
            s.dma("sp", t[:], d, w=[nm])
        gcount = 0
        for ti, (c0, n, isc) in enumerate(tiles or TILES2):
            nst = n // 128
            gi = 1 if isc else 0
            s.dma("sp", hbt[:, :, :n], h2r[:, :, c0:c0 + n], w=["hbt"])
            s.dma("sp", x1t[:, :, :n], x1r[:, :, c0:c0 + n], w=["x1t"])
            for st in range(nst if do_g else 0):
                t0 = c0 + st * 128
                s.dma("sp", st_[:], sc_d[t0:t0 + 128, :], w=["st"])
                s.dma("sp", tcs[:], tc_d[t0:t0 + 128, :], w=["tcs"])
                s.I("dve", lambda e: e.tensor_scalar(out=negc[:], in0=tcs[:, 8:16], scalar1=-1.0, scalar2=None,
                                                     op0=ALU.mult), r=["tcs"], w=["negc"])
                for q in range(NQ):
                    G, gk = Gb[q % 2], "G%d" % (q % 2)
                    for h in range(8):
                        S, sk = Sb[gcount % 2], "S%d" % (gcount % 2)
                        E, ek = Eb[gcount % 2], "E%d" % (gcount % 2)
                        gcount += 1
                        s.I("pool", lambda e: e.tensor_tensor(
                            out=AP(S, 0, [[128, IQ], [1, 128]]),
                            in0=AP(st_, (2 * h) * 128 + q * IQ, [[1, IQ], [0, 128]]),
                            in1=AP(st_, (2 * h + 1) * 128, [[0, IQ], [1, 128]]), op=ALU.add),
                            r=["st"], w=[sk])
                        s.I("act", lambda e: e.activation(out=E[:], in_=S[:], func=AF.Exp, bias=negc[:, h:h + 1]),
                            r=[sk, "negc"], w=[ek])
                        if h == 0:
                            s.I("dve", lambda e: e.scalar_tensor_tensor(out=G[:], in0=S[:], scalar=tcs[:, h:h + 1],
                                                                        in1=E[:], op0=ALU.is_ge, op1=ALU.mult),
                                r=[sk, ek, "tcs"], w=[gk])
                        else:
                            s.I("dve", lambda e: e.scalar_tensor_tensor(out=Gh[:], in0=S[:], scalar=tcs[:, h:h + 1],
                                                                        in1=E[:], op0=ALU.is_ge, op1=ALU.mult),
                                r=[sk, ek, "tcs"], w=["Gh"])
                            s.I("dve", lambda e: e.tensor_tensor(out=G[:], in0=G[:], in1=Gh[:], op=ALU.add),
                                r=[gk, "Gh"], w=[gk])
                    for i0 in range(0, IQ, 8):
                        for il in range(i0, i0 + 8):
                            s.I("pe", lambda e: e.transpose(out=pT[:, (il - i0) * 128:(il - i0 + 1) * 128],
                                                            in_=G[:, il * 128:(il + 1) * 128], identity=identb[:]),
                                r=[gk, "identb"], w=["pT"], sig=(il == i0 + 7))
                        ch0 = q * IQ + i0
                        dst = AP(GT, ch0 * 256 + st * 128, [[256, 8], [1, 128]])
                        src = AP(pT, 0, [[128, 8], [1, 128]])
                        s.I("act", lambda e: e.activation(out=dst, in_=src, func=AF.Copy), r=["pT"], w=["GT"])
            if not do_g:
                s.I("pool", lambda e: e.memset(GT[:], 0.0), w=["GT"])
            for c in range(128 if do_e else 0):
                gsel = (c // 4) % 2
                if c % 4 == 0:
                    s.dma("sp", UG[gsel][:], ut_d[c:c + 4].rearrange("c p f -> p c f"), w=["UG%d" % gsel])
                    s.dma("pool", VG[gsel][:], vb_d[c:c + 4].rearrange("c p f -> p c f"), w=["VG%d" % gsel])
                U, V = UG[gsel], VG[gsel]
                pa, pak = pact[c % 2], "pa%d" % (c % 2)
                for k in range(DC):
                    s.I("pe", lambda e: e.matmul(pa[:, :n], lhsT=U[:, c % 4, k * 128:(k + 1) * 128], rhs=hbt[:, k, :n],
                                                 start=(k == 0), stop=(k == DC - 1)),
                        r=["UG%d" % gsel, "hbt"], w=[pak], sig=(k == DC - 1))
                ge, gek = geb[c % 2], "ge%d" % (c % 2)
                pt, ptk = ptb[c % 2], "pt%d" % (c % 2)
                s.I("act", lambda e: e.activation(out=ge[:, :n], in_=pa[:, :n], func=gelu_func), r=[pak], w=[gek])
                s.I("dve", lambda e: e.tensor_tensor(out=pt[:, :n], in0=ge[:, :n], in1=GT[:, c, :n], op=ALU.mult),
                    r=[gek, "GT"], w=[ptk])
                for st in range(nst):
                    for hf in range(2):
                        pidx = st * 2 + hf
                        s.I("pe", lambda e: e.matmul(po[pidx][:], lhsT=pt[:, st * 128:(st + 1) * 128],
                                                     rhs=V[:, c % 4, hf * 512:(hf + 1) * 512],
                                                     start=(c == 0), stop=(c == 127)),
                            r=[ptk, "VG%d" % gsel], w=["po%d" % pidx], sig=(st == nst - 1 and hf == 1))
            for st in range(nst):
                s.I("act", lambda e: e.activation(out=osb[:, 0:512], in_=po[st * 2][:], func=AF.Copy),
                    r=["po%d" % (st * 2)], w=["osb"])
                s.I("dve", lambda e: e.tensor_copy(out=osb[:, 512:1024], in_=po[st * 2 + 1][:]),
                    r=["po%d" % (st * 2 + 1)], w=["osb"])
                for k in range(DC):
                    pf, pfk = pact[k % 2], "pa%d" % (k % 2)
                    s.I("pe", lambda e: e.transpose(out=pf[:, 0:128], in_=osb[:, k * 128:(k + 1) * 128], identity=identf[:]),
                        r=["osb", "identf"], w=[pfk])
                    s.I("dve", lambda e: e.scalar_tensor_tensor(out=x2t[:, k, st * 128:(st + 1) * 128], in0=pf[:, 0:128],
                                                                scalar=g2[:, gi, k:k + 1],
                                                                in1=x1t[:, k, st * 128:(st + 1) * 128],
                                                                op0=ALU.mult, op1=ALU.add),
                        r=[pfk, "g2", "x1t"], w=["x2t"])
            s.dma("sp", x2r[:, :, c0:c0 + n], x2t[:, :, :n], r=["x2t"])
        s.finish()
    return nc


NT = LC + L
NBLK = NT // 128


def build_attn():
    nc = new_nc()
    q_d = din(nc, "q", [8, 64, NT])
    k_d = din(nc, "k", [2, 64, NT])
    v_d = din(nc, "v", [2, NT, 64])
    gn_d = din(nc, "gn", [64, 2])
    cos_d = din(nc, "cos", [64, NT])
    sin_d = din(nc, "sin", [64, NT])
    rot_d = din(nc, "rot", [64, 64])
    ones_d = din(nc, "ones", [64, 64])
    mlo_d = din(nc, "mlo", [128, 128])
    mhi_d = din(nc, "mhi", [128, 128])
    sink_d = din(nc, "sink", [128, 8])
    o_d = dout(nc, "o", [NT, 512])
    with ExitStack() as es:
        s = Sched(nc, es)
        eps64 = s.sb("eps64", [64, 1])
        s.I("dve", lambda e: e.memset(eps64[:], EPS), w=["eps64"])
        cosT = s.sb("cosT", [64, NT])
        sinT = s.sb("sinT", [64, NT])
        rot = s.sb("rot", [64, 64])
        ones = s.sb("ones", [64, 64])
        gn = s.sb("gn", [64, 2])
        mlo = s.sb("mlo", [128, 128])
        mhi = s.sb("mhi", [128, 128])
        esink = s.sb("esink", [128, 8])
        qt = s.sb("qt", [64, 4, NT])
        kt = s.sb("kt", [64, NT])
        vt = s.sb("vt", [128, NBLK, 65])
        sq = s.sb("sq", [64, 512])
        rstd = s.sb("rstd", [64, 512])
        xn = s.sb("xn", [64, 512])
        t1 = s.sb("t1", [64, 512])
        t2 = s.sb("t2", [64, 512])
        E = s.sb("E", [128, 5, 512])
        den = s.sb("den", [128, 4])
        osb = [s.sb("osb%d" % i, [128, 4, 64]) for i in range(2)]
        pn = s.ps("pn", [64, 512])
        pr = s.ps("pr", [64, 512])
        psS = s.ps("psS", [128, 5 * 512])
        psO = s.ps("psO", [128, 512])
        for nm, t, d in (("cosT", cosT, cos_d), ("sinT", sinT, sin_d), ("rot", rot, rot_d), ("ones", ones, ones_d),
                         ("gn", gn, gn_d), ("mlo", mlo, mlo_d), ("mhi", mhi, mhi_d), ("esink", esink, sink_d)):
            s.dma("sp", t[:], d, w=[nm])
        s.I("act", lambda e: e.activation(out=esink[:], in_=esink[:], func=AF.Exp), r=["esink"], w=["esink"])
        s.I("dve", lambda e: e.memset(vt[:, :, 64:65], 1.0), w=["vt1"])
        cblocks = [(c0, min(512, NT - c0)) for c0 in range(0, NT, 512)]
        for j in range(2):
            s.dma("sp", kt[:], k_d[j], w=["kt"])
            for g in range(4):
                s.dma("sp", qt[:, g, :], q_d[j * 4 + g], w=["qt"])
            s.dma("sp", vt[:, :, 0:64], v_d[j].rearrange("(n p) d -> p n d", p=128), w=["vt"])
            for hi in range(5):
                gcol = 1 if hi == 0 else 0
                for (c0, n) in cblocks:
                    if hi == 0:
                        xs = kt[:, c0:c0 + n]
                        xk = "kt"
                    else:
                        xs = qt[:, hi - 1, c0:c0 + n]
                        xk = "qt"
                    s.I("act", lambda e: e.activation(out=sq[:, :n], in_=xs, func=AF.Square), r=[xk], w=["sq"])
                    s.I("pe", lambda e: e.matmul(pn[:, :n], lhsT=ones[:], rhs=sq[:, :n], start=True, stop=True),
                        r=["sq", "ones"], w=["pn"])
                    s.I("act", lambda e: e.activation(out=rstd[:, :n], in_=pn[:, :n], func=AF.Sqrt, scale=1.0 / 64,
                                                      bias=eps64[:]), r=["pn", "eps64"], w=["rstd"])
                    s.I("dve", lambda e: e.reciprocal(out=rstd[:, :n], in_=rstd[:, :n]), r=["rstd"], w=["rstd"])
                    s.I("dve", lambda e: e.scalar_tensor_tensor(out=xn[:, :n], in0=xs, scalar=gn[:, gcol:gcol + 1],
                                                                in1=rstd[:, :n], op0=ALU.mult, op1=ALU.mult),
                        r=[xk, "gn", "rstd"], w=["xn"])
                    s.I("pe", lambda e: e.matmul(pr[:, :n], lhsT=rot[:], rhs=xn[:, :n], start=True, stop=True),
                        r=["xn", "rot"], w=["pr"])
                    s.I("dve", lambda e: e.tensor_tensor(out=t1[:, :n], in0=xn[:, :n], in1=cosT[:, c0:c0 + n], op=ALU.mult),
                        r=["xn", "cosT"], w=["t1"])
                    s.I("dve", lambda e: e.tensor_tensor(out=t2[:, :n], in0=pr[:, :n], in1=sinT[:, c0:c0 + n], op=ALU.mult),
                        r=["pr", "sinT"], w=["t2"])
                    s.I("dve", lambda e: e.tensor_tensor(out=xs, in0=t1[:, :n], in1=t2[:, :n], op=ALU.add),
                        r=["t1", "t2"], w=[xk])
            for qb in range(NBLK):
                if qb < 2:
                    kbs = [(0, None), (1, None)]
                else:
                    kbs = []
                    if qb - 1 >= 2:
                        kbs.append((qb - 1, mlo))
                    kbs.append((qb, None))
                    if qb + 1 < NBLK:
                        kbs.append((qb + 1, mhi))
                    kbs += [(0, None), (1, None)]
                nk = len(kbs)
                qcols = AP(qt, qb * 128, [[NT, 4], [1, 128]], nparts=64)
                for si, (kb, msk) in enumerate(kbs):
                    s.I("pe", lambda e: e.matmul(psS[:, si * 512:(si + 1) * 512], lhsT=kt[:, kb * 128:(kb + 1) * 128],
                                                 rhs=qcols, start=True, stop=True),
                        r=["kt", "qt"], w=["psS"], sig=(si == nk - 1))
                s.I("act", lambda e: e.activation(out=E[:, 0:nk, :], in_=psS[:, 0:nk * 512], func=AF.Exp, scale=0.125),
                    r=["psS"], w=["E"])
                for si, (kb, msk) in enumerate(kbs):
                    if msk is not None:
                        mk = "mlo" if msk is mlo else "mhi"
                        s.I("dve", lambda e: e.tensor_tensor(out=AP(E, si * 512, [[128, 4], [1, 128]]),
                                                             in0=AP(E, si * 512, [[128, 4], [1, 128]]),
                                                             in1=AP(msk, 0, [[0, 4], [1, 128]]), op=ALU.mult),
                            r=["E", mk], w=["E"])
                for g in range(4):
                    for si, (kb, msk) in enumerate(kbs):
                        s.I("pe", lambda e: e.matmul(psO[:, g * 65:(g + 1) * 65], lhsT=E[:, si, g * 128:(g + 1) * 128],
                                                     rhs=vt[:, kb, :], start=(si == 0), stop=(si == nk - 1)),
                            r=["E", "vt", "vt1"], w=["psO"], sig=(g == 3 and si == nk - 1))
                ob, obk = osb[qb % 2], "osb%d" % (qb % 2)
                s.I("dve", lambda e: e.tensor_tensor(out=den[:], in0=AP(psO, 64, [[65, 4]]), in1=esink[:, j * 4:(j + 1) * 4],
                                                     op=ALU.add), r=["psO", "esink"], w=["den"])
                s.I("dve", lambda e: e.reciprocal(out=den[:], in_=den[:]), r=["den"], w=["den"])
                s.I("dve", lambda e: e.tensor_tensor(out=ob[:], in0=AP(psO, 0, [[65, 4], [1, 64]]),
                                                     in1=AP(den, 0, [[1, 4], [0, 64]]), op=ALU.mult),
                    r=["psO", "den"], w=[obk])
                s.dma("sp", o_d[qb * 128:(qb + 1) * 128, j * 256:(j + 1) * 256], ob[:], r=[obk])
        s.finish()
    return nc


def rope_tables():
    t = np.arange(L)
    row = (t // 64).astype(np.float32)
    col = (t % 64).astype(np.float32)
    freqs = (np.float32(10000.0) ** (-np.arange(16, dtype=np.float32) / np.float32(16))).astype(np.float32)
    ang = np.stack([row[:, None] * freqs, col[:, None] * freqs], 1)
    cos = np.cos(ang).astype(np.float32)
    sin = np.sin(ang).astype(np.float32)
    cosT = np.ones((64, NT), np.float32)
    sinT = np.zeros((64, NT), np.float32)
    for half in range(2):
        for pair in range(2):
            d0 = half * 32 + pair * 16
            cosT[d0:d0 + 16, LC:] = cos[:, half, :].T
            sinT[d0:d0 + 16, LC:] = sin[:, half, :].T
    rot = np.zeros((64, 64), np.float32)
    for half in range(2):
        for f in range(16):
            d1 = half * 32 + f
            d2 = d1 + 16
            rot[d2, d1] = -1.0
            rot[d1, d2] = 1.0
    return cosT, sinT, rot


def attn_consts():
    cosT, sinT, rot = rope_tables()
    kk = np.arange(128)[:, None]
    qq = np.arange(128)[None, :]
    return dict(cos=cosT, sin=sinT, rot=rot, ones=np.ones((64, 64), np.float32),
                mlo=(kk >= qq).astype(np.float32), mhi=(kk <= qq).astype(np.float32))


def build_lru():
    nc = new_nc()
    g_d = din(nc, "g", [512, NT])
    u_d = din(nc, "u", [512, NT])
    cw_d = din(nc, "cw", [128, 4, 4])
    cb_d = din(nc, "cb", [128, 4])
    wa_d = din(nc, "wa", [128, 2, 4, 128])
    wx_d = din(nc, "wx", [128, 2, 4, 128])
    ba_d = din(nc, "ba", [128, 2, 4])
    bx_d = din(nc, "bx", [128, 2, 4])
    lam_d = din(nc, "lam", [128, 2, 4])
    z_d = dout(nc, "z", [512, NT])
    with ExitStack() as es:
        s = Sched(nc, es)
        one = s.sb("one", [128, 1])
        s.I("dve", lambda e: e.memset(one[:], 1.0), w=["one"])
        cw = s.sb("cw", [128, 4, 4])
        cb = s.sb("cb", [128, 4])
        wa = s.sb("wa", [128, 2, 4, 128])
        wx = s.sb("wx", [128, 2, 4, 128])
        ba = s.sb("ba", [128, 2, 4])
        bx = s.sb("bx", [128, 2, 4])
        lam = s.sb("lam", [128, 2, 4])
        m8 = s.sb("m8", [128, 2, 4])
        m16 = s.sb("m16", [128, 2, 4])
        gt = s.sb("gt", [128, NT])
        upc = s.sb("upc", [128, LC + 3])
        upl = s.sb("upl", [128, L + 3])
        u = s.sb("u", [128, NT])
        rr = s.sb("rr", [128, NT])
        ig = s.sb("ig", [128, NT])
        aa = s.sb("aa", [128, NT])
        bt = s.sb("bt", [128, NT])
        hs = s.sb("hs", [128, NT])
        hsum = s.sb("hsum", [128, NT])
        pps = [s.ps("pp%d" % i, [128, 512]) for i in range(4)]
        for nm, t, d in (("cw", cw, cw_d), ("cb", cb, cb_d), ("wa", wa, wa_d), ("wx", wx, wx_d), ("ba", ba, ba_d),
                         ("bx", bx, bx_d), ("lam", lam, lam_d)):
            s.dma("sp", t[:], d, w=[nm])
        s.I("act", lambda e: e.activation(out=m8[:], in_=lam[:], func=AF.Exp, scale=-1.0), r=["lam"], w=["m8"])
        s.I("act", lambda e: e.activation(out=m8[:], in_=m8[:], func=AF.Ln, bias=one[:]), r=["m8", "one"], w=["m8"])
        s.I("dve", lambda e: e.tensor_scalar(out=m16[:], in0=m8[:], scalar1=-16.0, scalar2=None, op0=ALU.mult),
            r=["m8"], w=["m16"])
        s.I("dve", lambda e: e.tensor_scalar(out=m8[:], in0=m8[:], scalar1=-8.0, scalar2=None, op0=ALU.mult),
            r=["m8", "m16"], w=["m8"])
        s.I("pool", lambda e: e.memset(upc[:], 0.0), w=["upc"])
        s.I("pool", lambda e: e.memset(upl[:], 0.0), w=["upl"])
        blocks = [(0, 256)] + [(256 + 512 * i, 512) for i in range(8)]
        segs = [(0, LC, upc, "upc"), (LC, L, upl, "upl")]
        pi = 0
        for c in range(4):
            rows = slice(c * 128, (c + 1) * 128)
            s.dma("sp", gt[:], g_d[rows, :], w=["gt"])
            s.dma("sp", upc[:, 2:2 + LC], u_d[rows, 0:LC], w=["upc"])
            s.dma("sp", upl[:, 2:2 + L], u_d[rows, LC:NT], w=["upl"])
            s.I("act", lambda e: e.activation(out=gt[:], in_=gt[:], func=AF.Gelu_apprx_tanh), r=["gt"], w=["gt"])
            for (o0, ln, up, upk) in segs:
                s.I("dve", lambda e: e.tensor_scalar(out=u[:, o0:o0 + ln], in0=up[:, 0:ln], scalar1=cw[:, c, 0:1],
                                                     scalar2=cb[:, c:c + 1], op0=ALU.mult, op1=ALU.add),
                    r=[upk, "cw", "cb"], w=["u"])
                for k in range(1, 4):
                    s.I("dve", lambda e: e.scalar_tensor_tensor(out=u[:, o0:o0 + ln], in0=up[:, k:k + ln],
                                                                scalar=cw[:, c, k:k + 1], in1=u[:, o0:o0 + ln],
                                                                op0=ALU.mult, op1=ALU.add),
                        r=[upk, "cw", "u"], w=["u"])
            for d in range(2):
                for (c0, n) in blocks:
                    pa_, pak = pps[pi % 4], "pp%d" % (pi % 4)
                    pi += 1
                    px_, pxk = pps[pi % 4], "pp%d" % (pi % 4)
                    pi += 1
                    s.I("pe", lambda e: e.matmul(pa_[:, :n], lhsT=wa[:, d, c, :], rhs=u[:, c0:c0 + n], start=True, stop=True),
                        r=["wa", "u"], w=[pak])
                    s.I("pe", lambda e: e.matmul(px_[:, :n], lhsT=wx[:, d, c, :], rhs=u[:, c0:c0 + n], start=True, stop=True),
                        r=["wx", "u"], w=[pxk])
                    s.I("act", lambda e: e.activation(out=rr[:, c0:c0 + n], in_=pa_[:, :n], func=AF.Sigmoid,
                                                      bias=ba[:, d, c:c + 1]), r=[pak, "ba"], w=["rr"])
                    s.I("act", lambda e: e.activation(out=ig[:, c0:c0 + n], in_=px_[:, :n], func=AF.Sigmoid,
                                                      bias=bx[:, d, c:c + 1]), r=[pxk, "bx"], w=["ig"])
                s.I("act", lambda e: e.activation(out=aa[:], in_=rr[:], func=AF.Exp, scale=m8[:, d, c:c + 1]),
                    r=["rr", "m8"], w=["aa"])
                s.I("act", lambda e: e.activation(out=bt[:], in_=rr[:], func=AF.Exp, scale=m16[:, d, c:c + 1]),
                    r=["rr", "m16"], w=["bt"])
                s.I("act", lambda e: e.activation(out=bt[:], in_=bt[:], func=AF.Sqrt, scale=-1.0, bias=one[:]),
                    r=["bt", "one"], w=["bt"])
                s.I("dve", lambda e: e.tensor_tensor(out=ig[:], in0=ig[:], in1=u[:], op=ALU.mult), r=["ig", "u"], w=["ig"])
                s.I("dve", lambda e: e.tensor_tensor(out=bt[:], in0=bt[:], in1=ig[:], op=ALU.mult), r=["bt", "ig"], w=["bt"])
                if d == 0:
                    s.I("dve", lambda e: e.tensor_tensor_scan(out=hs[:, 0:LC], data0=aa[:, 0:LC], data1=bt[:, 0:LC],
                                                              initial=0.0, op0=ALU.mult, op1=ALU.add),
                        r=["aa", "bt"], w=["hs"])
                    s.I("dve", lambda e: e.tensor_tensor_scan(out=hs[:, LC:NT], data0=aa[:, LC:NT], data1=bt[:, LC:NT],
                                                              initial=hs[:, LC - 1:LC], op0=ALU.mult, op1=ALU.add),
                        r=["aa", "bt", "hs"], w=["hs"])
                    s.I("pool", lambda e: e.tensor_copy(out=hsum[:], in_=hs[:]), r=["hs"], w=["hsum"])
                else:
                    rc = lambda t: AP(t, LC - 1, [[-1, LC]])
                    rl = lambda t: AP(t, NT - 1, [[-1, L]])
                    s.I("dve", lambda e: e.tensor_tensor_scan(out=rc(hs), data0=rc(aa), data1=rc(bt),
                                                              initial=0.0, op0=ALU.mult, op1=ALU.add),
                        r=["aa", "bt"], w=["hs"])
                    s.I("dve", lambda e: e.tensor_tensor_scan(out=rl(hs), data0=rl(aa), data1=rl(bt),
                                                              initial=hs[:, 0:1], op0=ALU.mult, op1=ALU.add),
                        r=["aa", "bt", "hs"], w=["hs"])
                    s.I("pool", lambda e: e.tensor_tensor(out=hsum[:], in0=hsum[:], in1=hs[:], op=ALU.add),
                        r=["hs", "hsum"], w=["hsum"])
            s.I("dve", lambda e: e.tensor_tensor(out=hsum[:], in0=hsum[:], in1=gt[:], op=ALU.mult), r=["hsum", "gt"], w=["hsum"])
            s.dma("sp", z_d[rows, :], hsum[:], r=["hsum"])
        s.finish()
    return nc


def lru_params(inp, s):
    ch = slice(512 * s, 512 * (s + 1))
    pl = lambda v: np.ascontiguousarray(np.asarray(v)[..., ch].reshape(v.shape[:-1] + (4, 128)))
    cw = np.ascontiguousarray(pl(inp["lru_conv_w"][0]).transpose(2, 1, 0))
    cb = np.ascontiguousarray(pl(inp["lru_conv_b"][0]).T)
    wa = np.ascontiguousarray(inp["lru_w_a"][0][:, 4 * s:4 * s + 4].transpose(2, 0, 1, 3))
    wx = np.ascontiguousarray(inp["lru_w_x"][0][:, 4 * s:4 * s + 4].transpose(2, 0, 1, 3))
    t3 = lambda v: np.ascontiguousarray(pl(v).transpose(2, 0, 1))
    return dict(cw=cw, cb=cb, wa=wa, wx=wx, ba=t3(inp["lru_b_a"][0]), bx=t3(inp["lru_b_x"][0]), lam=t3(inp["lru_lambda"][0]))


def build_conf():
    nc = new_nc()
    HAL = 30
    vgc_d = din(nc, "vgc", [2048, 128 + HAL])
    vgl_d = din(nc, "vgl", [2048, 2048 + HAL])
    dw_d = din(nc, "dw", [128, 8, 31])
    dwb_d = din(nc, "dwb", [128, 8])
    lng_d = din(nc, "lng", [128, 8])
    lnb_d = din(nc, "lnb", [128, 8])
    ones_d = din(nc, "ones", [128, 128])
    z_d = dout(nc, "z", [D, TOK])
    with ExitStack() as es:
        s = Sched(nc, es)
        emit_consts(s)
        dw = s.sb("dw", [128, 8, 31])
        dwb = s.sb("dwb", [128, 8])
        lng = s.sb("lng", [128, 8])
        lnb = s.sb("lnb", [128, 8])
        ones = s.sb("ones", [128, 128])
        U = s.sb("U", [128, 8, 2048])
        val = [s.sb("val%d" % i, [128, 2048 + HAL]) for i in range(2)]
        gate = [s.sb("gate%d" % i, [128, 2048 + HAL]) for i in range(2)]
        sq = s.sb("sq", [128, 8, 512])
        mean = s.sb("mean", [128, 512])
        msq = s.sb("msq", [128, 512])
        rstd = s.sb("rstd", [128, 512])
        tt = [s.sb("tt%d" % i, [128, 512]) for i in range(2)]
        zo = [s.sb("zo%d" % i, [128, 512]) for i in range(2)]
        ps1 = s.ps("ps1", [128, 512])
        ps2 = s.ps("ps2", [128, 512])
        for nm, t, d in (("dw", dw, dw_d), ("dwb", dwb, dwb_d), ("lng", lng, lng_d), ("lnb", lnb, lnb_d), ("ones", ones, ones_d)):
            s.dma("sp", t[:], d, w=[nm])
        zi = 0
        for (src, Ls, ocol) in ((vgc_d, 128, 0), (vgl_d, 2048, 128)):
            W = Ls + HAL
            for k in range(8):
                v, vk = val[k % 2], "val%d" % (k % 2)
                g, gk = gate[k % 2], "gate%d" % (k % 2)
                s.dma("sp", v[:, :W], src[k * 128:(k + 1) * 128, :], w=[vk])
                s.dma("sp", g[:, :W], src[1024 + k * 128:1024 + (k + 1) * 128, :], w=[gk])
                s.I("act", lambda e: e.activation(out=g[:, :W], in_=g[:, :W], func=AF.Sigmoid), r=[gk], w=[gk])
                s.I("pool", lambda e: e.tensor_tensor(out=v[:, :W], in0=v[:, :W], in1=g[:, :W], op=ALU.mult), r=[vk, gk], w=[vk])
                s.I("dve", lambda e: e.tensor_scalar(out=U[:, k, :Ls], in0=v[:, 0:Ls], scalar1=dw[:, k, 0:1],
                                                     scalar2=dwb[:, k:k + 1], op0=ALU.mult, op1=ALU.add),
                    r=[vk, "dw", "dwb"], w=["U%d" % k])
                for j in range(1, 31):
                    s.I("dve", lambda e: e.scalar_tensor_tensor(out=U[:, k, :Ls], in0=v[:, j:j + Ls], scalar=dw[:, k, j:j + 1],
                                                                in1=U[:, k, :Ls], op0=ALU.mult, op1=ALU.add),
                        r=[vk, "dw", "U%d" % k], w=["U%d" % k])
            for c0 in range(0, Ls, 512):
                n = min(512, Ls - c0)
                uk = ["U%d" % k for k in range(8)]
                s.I("act", lambda e: e.activation(out=sq[:, :, :n], in_=U[:, :, c0:c0 + n], func=AF.Square), r=uk, w=["sq"])
                for k in range(8):
                    s.I("pe", lambda e: e.matmul(ps1[:, :n], lhsT=ones[:], rhs=U[:, k, c0:c0 + n], start=(k == 0), stop=(k == 7)),
                        r=["U%d" % k, "ones"], w=["ps1"], sig=(k == 7))
                for k in range(8):
                    s.I("pe", lambda e: e.matmul(ps2[:, :n], lhsT=ones[:], rhs=sq[:, k, :n], start=(k == 0), stop=(k == 7)),
                        r=["sq", "ones"], w=["ps2"], sig=(k == 7))
                s.I("act", lambda e: e.activation(out=mean[:, :n], in_=ps1[:, :n], func=AF.Copy, scale=1.0 / D), r=["ps1"], w=["mean"])
                s.I("dve", lambda e: e.tensor_tensor(out=msq[:, :n], in0=mean[:, :n], in1=mean[:, :n], op=ALU.mult),
                    r=["mean"], w=["msq"])
                s.I("dve", lambda e: e.scalar_tensor_tensor(out=rstd[:, :n], in0=ps2[:, :n], scalar=1.0 / D, in1=msq[:, :n],
                                                            op0=ALU.mult, op1=ALU.subtract), r=["ps2", "msq"], w=["rstd"])
                s.I("act", lambda e: e.activation(out=rstd[:, :n], in_=rstd[:, :n], func=AF.Sqrt, bias=s.epsb[:]),
                    r=["rstd", "epsb"], w=["rstd"])
                s.I("dve", lambda e: e.reciprocal(out=rstd[:, :n], in_=rstd[:, :n]), r=["rstd"], w=["rstd"])
                for k in range(8):
                    t_, tk = tt[k % 2], "tt%d" % (k % 2)
                    z_, zk = zo[zi % 2], "zo%d" % (zi % 2)
                    zi += 1
                    s.I("dve", lambda e: e.tensor_tensor(out=t_[:, :n], in0=U[:, k, c0:c0 + n], in1=mean[:, :n], op=ALU.subtract),
                        r=["U%d" % k, "mean"], w=[tk])
                    s.I("pool", lambda e: e.tensor_tensor(out=t_[:, :n], in0=t_[:, :n], in1=rstd[:, :n], op=ALU.mult),
                        r=[tk, "rstd"], w=[tk])
                    s.I("act", lambda e: e.activation(out=z_[:, :n], in_=t_[:, :n], func=AF.Silu, scale=lng[:, k:k + 1],
                                                      bias=lnb[:, k:k + 1]), r=[tk, "lng", "lnb"], w=[zk])
                    s.dma("sp", z_d[k * 128:(k + 1) * 128, ocol + c0:ocol + c0 + n], z_[:, :n], r=[zk])
        s.finish()
    return nc


_PROGS = {}


def _prog(name, fn, *a):
    key = (name,) + a
    if key not in _PROGS:
        _PROGS[key] = fn(*a)
    return _PROGS[key]


def _halo(seg, s, Ls):
    Lt = seg.shape[1]
    out = np.zeros((seg.shape[0], Ls + 30), np.float32)
    a, b = s * Ls - 15, (s + 1) * Ls + 15
    a2, b2 = max(a, 0), min(b, Lt)
    out[:, a2 - a:a2 - a + (b2 - a2)] = seg[:, a2:b2]
    return out


def kernel(**inp):
    import ml_dtypes
    inp = {k: np.asarray(v) for k, v in inp.items()}
    x, ctx, ng = inp["x"], inp["ctx"], inp["norm_g"]
    mod = run_ada(inp)
    ones128 = np.ones((128, 128), np.float32)
    identb = np.eye(128, dtype=ml_dtypes.bfloat16)
    identf = np.eye(128, dtype=np.float32)
    acst = attn_consts()
    xs = []
    for c in range(NCORES):
        b, h = c // 2, c % 2
        xs.append(np.ascontiguousarray(np.concatenate([ctx[b, h * 128:(h + 1) * 128], x[b, h * 2048:(h + 1) * 2048]], 0).T))
    for layer in range(4):
        kind, slot = layer % 3, layer // 3
        if kind == 0:
            W, bias = inp["attn_wqkv"][slot], np.zeros(1536, np.float32)
        elif kind == 1:
            W, bias = inp["lru_w_in"][0], np.zeros(2048, np.float32)
        else:
            W, bias = inp["conf_w_pw1"][0], inp["conf_b_pw1"][0]
        N = W.shape[1]
        W = np.ascontiguousarray(W)
        biasl = np.ascontiguousarray(bias.reshape(-1, 128).T)
        maps = [{"x": xs[c], "w": W, "mods": make_mods(mod, ng, layer, c // 2, 0), "bias": biasl, "ones": ones128}
                for c in range(NCORES)]
        ys = [r["y"] for r in run(_prog("pre", build_pre, N), maps)]
        yb = [np.concatenate([ys[2 * b][:, :128], ys[2 * b + 1][:, :128], ys[2 * b][:, 128:], ys[2 * b + 1][:, 128:]], 1)
              for b in range(B)]
        if kind == 0:
            maps = []
            for c in range(NCORES):
                b, s = c // 2, c % 2
                q = yb[b][0:1024].reshape(16, 64, NT)[8 * s:8 * s + 8]
                k = yb[b][1024:1280].reshape(4, 64, NT)[2 * s:2 * s + 2]
                v = yb[b][1280:1536].reshape(4, 64, NT)[2 * s:2 * s + 2].transpose(0, 2, 1)
                gn = np.stack([inp["attn_q_gain"][slot], inp["attn_k_gain"][slot]], 1)
                sink = np.broadcast_to(inp["attn_sink"][slot][8 * s:8 * s + 8][None, :], (128, 8))
                m = dict(q=np.ascontiguousarray(q), k=np.ascontiguousarray(k), v=np.ascontiguousarray(v),
                         gn=np.ascontiguousarray(gn), sink=np.ascontiguousarray(sink))
                m.update(acst)
                maps.append(m)
            os_ = [r["o"] for r in run(_prog("attn", build_attn), maps)]
            yin = []
            for c in range(NCORES):
                b, h = c // 2, c % 2
                ob = np.concatenate([os_[2 * b], os_[2 * b + 1]], 1)
                yin.append(np.ascontiguousarray(
                    np.concatenate([ob[h * 128:(h + 1) * 128], ob[LC + h * 2048:LC + (h + 1) * 2048]], 0).T))
            wout, bout = inp["attn_wo"][slot], np.zeros(D, np.float32)
        elif kind == 1:
            maps = []
            for c in range(NCORES):
                b, s = c // 2, c % 2
                m = dict(g=np.ascontiguousarray(yb[b][512 * s:512 * s + 512]),
                         u=np.ascontiguousarray(yb[b][1024 + 512 * s:1024 + 512 * s + 512]))
                m.update(lru_params(inp, s))
                maps.append(m)
            zs = [r["z"] for r in run(_prog("lru", build_lru), maps)]
            yin = []
            for c in range(NCORES):
                b, h = c // 2, c % 2
                zb = np.concatenate([zs[2 * b], zs[2 * b + 1]], 0)
                yin.append(np.ascontiguousarray(
                    np.concatenate([zb[:, h * 128:(h + 1) * 128], zb[:, LC + h * 2048:LC + (h + 1) * 2048]], 1)))
            wout, bout = inp["lru_w_out"][0], np.zeros(D, np.float32)
        else:
            cp = dict(dw=np.ascontiguousarray(inp["conf_dw_w"][0].reshape(31, 8, 128).transpose(2, 1, 0)),
                      dwb=pm(inp["conf_dw_b"][0]), lng=pm(inp["conf_ln_g"][0]), lnb=pm(inp["conf_ln_b"][0]), ones=ones128)
            maps = []
            for c in range(NCORES):
                b, s = c // 2, c % 2
                m = dict(vgc=_halo(yb[b][:, :LC], s, 128), vgl=_halo(yb[b][:, LC:], s, 2048))
                m.update(cp)
                maps.append(m)
            yin = [r["z"] for r in run(_prog("conf", build_conf), maps)]
            wout, bout = inp["conf_w_pw2"][0], inp["conf_b_pw2"][0]
        u, v = inp["peer_u"][layer], inp["peer_v"][layer]
        utf = np.ascontiguousarray(u.reshape(128, 128, 8, 128).transpose(0, 3, 2, 1).reshape(128, 128, 1024))
        vf = v.reshape(128, 128, 1024)
        res = run(_prog("wcast", build_wcast), [{"uf": utf[16 * c:16 * c + 16], "vf": vf[16 * c:16 * c + 16]}
                                                for c in range(NCORES)])
        ub = np.concatenate([np.asarray(r["ub"]) for r in res], 0)
        vb = np.concatenate([np.asarray(r["vb"]) for r in res], 0)
        del utf
        keys = np.zeros((128, 16, 128), np.float32)
        for h in range(8):
            keys[:, 2 * h, :] = inp["peer_keys1"][layer][h].T
            keys[:, 2 * h + 1, :] = inp["peer_keys2"][layer][h].T
        wq = np.ascontiguousarray(inp["peer_wq"][layer])
        wout = np.ascontiguousarray(wout)
        maps = []
        for c in range(NCORES):
            b = c // 2
            ml = mod[layer, b].reshape(6, D)
            mc = mod[layer, 4].reshape(6, D)
            maps.append({"x": xs[c], "yin": yin[c], "wout": wout, "bout": pm(bout), "mods": make_mods(mod, ng, layer, b, 1),
                         "gates": np.ascontiguousarray(np.stack([pm(ml[2]), pm(mc[2])], 1)), "wq": wq, "keys": keys,
                         "ones": ones128})
        r1 = run(_prog("c1", build_c1), maps)
        maps = []
        for c in range(NCORES):
            b = c // 2
            ml = mod[layer, b].reshape(6, D)
            mc = mod[layer, 4].reshape(6, D)
            maps.append({"x1": r1[c]["x1"], "h2b": r1[c]["h2b"], "sc": r1[c]["sc"], "tc": r1[c]["tc"],
                         "g2": np.ascontiguousarray(np.stack([pm(ml[5]), pm(mc[5])], 1)), "ut": ub, "vb": vb,
                         "identb": identb, "identf": identf})
        r2 = run(_prog("c2", build_c2), maps)
        xs = [np.asarray(r["x2"]) for r in r2]
    out = np.zeros((B, L, D), np.float32)
    for c in range(NCORES):
        b, h = c // 2, c % 2
        out[b, h * 2048:(h + 1) * 2048] = xs[c][:, 128:].T
    return out
```

```python
import numpy as np
import concourse.bass as bass
import concourse.mybir as mybir
from concourse.bass_utils import run_bass_kernel_spmd
from contextlib import ExitStack

F32 = mybir.dt.float32
BF16 = mybir.dt.bfloat16
U32 = mybir.dt.uint32
AF = mybir.ActivationFunctionType
ALU = mybir.AluOpType
AX = mybir.AxisListType

NCORES = 8
D = 1024
DC = 8
B = 4
L = 4096
LC = 256
TOK = 2176
EPS = 1e-6


class Sched:
    ENG = ("pe", "dve", "act", "pool", "sp")

    def __init__(self, nc, es, n_dma_sems=32):
        self.nc = nc
        self.es = es
        self.e = dict(pe=nc.tensor, dve=nc.vector, act=nc.scalar, pool=nc.gpsimd, sp=nc.sync)
        self.sem = {k: es.enter_context(nc.semaphore("s_" + k)) for k in self.ENG}
        self.cnt = {k: 0 for k in self.ENG}
        self.dsem = [es.enter_context(nc.semaphore("d%d" % i)) for i in range(n_dma_sems)]
        self.dcnt = [0] * n_dma_sems
        self.dnext = 0
        self.waited = {}
        self.lastw = {}
        self.readers = {}
        self.ntiles = 0

    def sb(self, name, shape, dtype=F32):
        return self.es.enter_context(self.nc.sbuf_tensor("t_" + name, list(shape), dtype))

    def ps(self, name, shape, dtype=F32):
        return self.es.enter_context(self.nc.psum_tensor("p_" + name, list(shape), dtype))

    def _wait(self, eng, dep):
        semkey, val = dep
        if semkey == ("e", "pe") and eng == "pe":
            return
        k = (eng, semkey)
        if self.waited.get(k, 0) >= val:
            return
        self.waited[k] = val
        sem = self.sem[semkey[1]] if semkey[0] == "e" else self.dsem[semkey[1]]
        self.e[eng].wait_ge(sem, val)

    def _deps(self, r, w):
        deps = []
        for k in r:
            if k in self.lastw:
                deps.append(self.lastw[k])
        for k in w:
            if k in self.lastw:
                deps.append(self.lastw[k])
            deps.extend(self.readers.get(k, {}).items())
        return deps

    def _record(self, tok, r, w):
        for k in r:
            d = self.readers.setdefault(k, {})
            if d.get(tok[0], 0) < tok[1]:
                d[tok[0]] = tok[1]
        for k in w:
            self.lastw[k] = tok
            self.readers[k] = {}

    def I(self, eng, fn, r=(), w=(), sig=True):
        for d in self._deps(r, w):
            self._wait(eng, d)
        ins = fn(self.e[eng])
        if sig:
            self.cnt[eng] += 1
            ins.then_inc(self.sem[eng], 1)
            tok = (("e", eng), self.cnt[eng])
        else:
            tok = (("e", eng), self.cnt[eng] + 1)
        self._record(tok, r, w)
        return ins

    def dma(self, q, out, in_, r=(), w=()):
        if q == "pool":
            q = "sp"
        i = self.dnext
        self.dnext = (self.dnext + 1) % len(self.dsem)
        if self.dcnt[i] > 0:
            self._wait(q, (("d", i), self.dcnt[i]))
        for d in self._deps(r, w):
            self._wait(q, d)
        ins = self.e[q].dma_start(out=out, in_=in_)
        self.dcnt[i] += 16
        ins.then_inc(self.dsem[i], 16)
        self._record((("d", i), self.dcnt[i]), r, w)
        return ins

    def finish(self):
        for i, c in enumerate(self.dcnt):
            if c > 0:
                self._wait("sp", (("d", i), c))
        for k in self.ENG:
            if k != "sp" and self.cnt[k] > 0:
                self._wait("sp", (("e", k), self.cnt[k]))


def AP(t, offset, dims, nparts=128):
    base = t[:].ap[0][0]
    return bass.AP(tensor=t, offset=offset, ap=[[base, nparts]] + [list(d) for d in dims])


def new_nc():
    return bass.Bass("TRN2", target_bir_lowering=False)


def din(nc, name, shape, dtype=F32):
    return nc.dram_tensor(name, list(shape), dtype, kind="ExternalInput").ap()


def dout(nc, name, shape, dtype=F32):
    return nc.dram_tensor(name, list(shape), dtype, kind="ExternalOutput").ap()


def run(nc, in_maps):
    res = run_bass_kernel_spmd(nc, in_maps, core_ids=list(range(len(in_maps))))
    return res.results


def build_ada():
    nc = new_nc()
    NH = 3072
    w = din(nc, "w", [D, NH])
    cv = din(nc, "cv", [128, DC, 5])
    bb = din(nc, "b", [128, NH // 128])
    o = dout(nc, "o", [128, NH // 128, 5])
    with ExitStack() as es:
        s = Sched(nc, es)
        wt = s.sb("wt", [128, DC, NH])
        cvt = s.sb("cvt", [128, DC, 5])
        sv = s.sb("sv", [128, DC, 5])
        bt = s.sb("bt", [128, NH // 128])
        ot = s.sb("ot", [128, NH // 128, 5])
        pp = [s.ps("pp%d" % i, [128, 512]) for i in range(2)]
        for k in range(DC):
            s.dma("sp" if k % 2 == 0 else "pool", wt[:, k, :], w[k * 128:(k + 1) * 128, :], w=["wt%d" % k])
        s.dma("sp", cvt[:], cv, w=["cvt"])
        s.dma("sp", bt[:], bb, w=["bt"])
        s.I("act", lambda e: e.activation(out=sv[:], in_=cvt[:], func=AF.Silu), r=["cvt"], w=["sv"])
        for j in range(NH // 128):
            p = pp[j % 2]
            pk = "pp%d" % (j % 2)
            for k in range(DC):
                s.I("pe", lambda e: e.matmul(p[:, 0:5], lhsT=wt[:, k, j * 128:(j + 1) * 128], rhs=sv[:, k, :],
                                             start=(k == 0), stop=(k == DC - 1)),
                    r=["wt%d" % k, "sv"], w=[pk], sig=(k == DC - 1))
            s.I("dve", lambda e: e.tensor_scalar(out=ot[:, j, :], in0=p[:, 0:5], scalar1=bt[:, j:j + 1], scalar2=None,
                                                 op0=ALU.add), r=[pk, "bt"], w=["ot"])
        s.dma("sp", o, ot[:], r=["ot"])
        s.finish()
    return nc


def run_ada(inp):
    nc = build_ada()
    cvec = np.concatenate([inp["c"], inp["c_ctx"][None, :]], 0)
    cv = np.ascontiguousarray(cvec.T.reshape(DC, 128, 5).transpose(1, 0, 2))
    maps = []
    for c in range(NCORES):
        l, h = c // 2, c % 2
        w = np.ascontiguousarray(inp["ada_w"][l][:, h * 3072:(h + 1) * 3072])
        b = np.ascontiguousarray(inp["ada_b"][l][h * 3072:(h + 1) * 3072].reshape(24, 128).T)
        maps.append({"w": w, "cv": cv, "b": b})
    res = run(nc, maps)
    mod = np.zeros((4, 5, 6144), np.float32)
    for c in range(NCORES):
        l, h = c // 2, c % 2
        o = res[c]["o"]
        mod[l, :, h * 3072:(h + 1) * 3072] = o.transpose(2, 1, 0).reshape(5, 3072)
    return mod


TILES = [(0, 128, 1), (128, 512, 0), (640, 512, 0), (1152, 512, 0), (1664, 512, 0)]


def emit_modprep(s, mods, ab):
    for i, src in enumerate((1, 3)):
        s.I("dve", lambda e: e.scalar_tensor_tensor(out=ab[:, i, :], in0=mods[:, src, :], scalar=1.0, in1=mods[:, 0, :],
                                                    op0=ALU.add, op1=ALU.mult), r=["mods"], w=["ab"])


def emit_normmod(s, xt, xkey, n, is_ctx, ht, hkey, sq, rstd, ones, ab, mods, pss, hb=None, hbkey=None):
    s.I("act", lambda e: e.activation(out=sq[:, :, :n], in_=xt[:, :, :n], func=AF.Square), r=[xkey], w=["sq"])
    for k in range(DC):
        s.I("pe", lambda e: e.matmul(pss[:, :n], lhsT=ones[:], rhs=sq[:, k, :n], start=(k == 0), stop=(k == DC - 1)),
            r=["sq", "ones"], w=["pss"], sig=(k == DC - 1))
    s.I("act", lambda e: e.activation(out=rstd[:, :n], in_=pss[:, :n], func=AF.Sqrt, scale=1.0 / D, bias=s.epsb[:]),
        r=["pss"], w=["rstd"])
    s.I("dve", lambda e: e.reciprocal(out=rstd[:, :n], in_=rstd[:, :n]), r=["rstd"], w=["rstd"])
    ai = 1 if is_ctx else 0
    shi = 4 if is_ctx else 2
    for k in range(DC):
        s.I("dve", lambda e: e.tensor_tensor(out=sq[:, k, :n], in0=xt[:, k, :n], in1=rstd[:, :n], op=ALU.mult),
            r=[xkey, "rstd"], w=["sq"])
        s.I("act", lambda e: e.activation(out=ht[:, k, :n], in_=sq[:, k, :n], func=AF.Identity,
                                          scale=ab[:, ai, k:k + 1], bias=mods[:, shi, k:k + 1]),
            r=["sq", "ab", "mods"], w=[hkey])
        if hb is not None:
            s.I("pool", lambda e: e.tensor_copy(out=hb[:, k, :n], in_=ht[:, k, :n]), r=[hkey], w=[hbkey])


def emit_consts(s):
    s.epsb = s.sb("epsb", [128, 1])
    s.I("dve", lambda e: e.memset(s.epsb[:], EPS), w=["epsb"])


def build_pre(N):
    nc = new_nc()
    NJ = N // 128
    x = din(nc, "x", [D, TOK])
    w = din(nc, "w", [D, N])
    mods_d = din(nc, "mods", [128, 5, 8])
    bias_d = din(nc, "bias", [128, NJ])
    ones_d = din(nc, "ones", [128, 128])
    y = dout(nc, "y", [N, TOK])
    xr = x.rearrange("(k p) t -> p k t", p=128)
    with ExitStack() as es:
        s = Sched(nc, es)
        emit_consts(s)
        wt = s.sb("wt", [128, DC, N])
        mods = s.sb("mods", [128, 5, 8])
        ab = s.sb("ab", [128, 2, 8])
        bias = s.sb("bias", [128, NJ])
        ones = s.sb("ones", [128, 128])
        xts = [s.sb("xt%d" % i, [128, DC, 512]) for i in range(2)]
        hts = [s.sb("ht%d" % i, [128, DC, 512]) for i in range(2)]
        sq = s.sb("sq", [128, DC, 512])
        rstd = s.sb("rstd", [128, 512])
        ots = [s.sb("ot%d" % i, [128, 512]) for i in range(4)]
        pss = s.ps("pss", [128, 512])
        pmm = [s.ps("pmm%d" % i, [128, 512]) for i in range(3)]
        s.dma("sp", mods[:], mods_d, w=["mods"])
        s.dma("sp", bias[:], bias_d, w=["bias"])
        s.dma("sp", ones[:], ones_d, w=["ones"])
        for k in range(DC):
            s.dma("pool" if k % 2 else "sp", wt[:, k, :], w[k * 128:(k + 1) * 128, :], w=["wt%d" % k])
        emit_modprep(s, mods, ab)
        oc = 0
        for ti, (c0, n, isc) in enumerate(TILES):
            xt, xk = xts[ti % 2], "xt%d" % (ti % 2)
            ht, hk = hts[ti % 2], "ht%d" % (ti % 2)
            s.dma("sp", xt[:, :, :n], xr[:, :, c0:c0 + n], w=[xk])
            emit_normmod(s, xt, xk, n, isc, ht, hk, sq, rstd, ones, ab, mods, pss)
            for j in range(NJ):
                p, pk = pmm[j % 3], "pmm%d" % (j % 3)
                for k in range(DC):
                    s.I("pe", lambda e: e.matmul(p[:, :n], lhsT=wt[:, k, j * 128:(j + 1) * 128], rhs=ht[:, k, :n],
                                                 start=(k == 0), stop=(k == DC - 1)),
                        r=["wt%d" % k, hk], w=[pk], sig=(k == DC - 1))
                ot, ok = ots[oc % 4], "ot%d" % (oc % 4)
                oc += 1
                if j % 2 == 0:
                    s.I("act", lambda e: e.activation(out=ot[:, :n], in_=p[:, :n], func=AF.Identity,
                                                      bias=bias[:, j:j + 1]), r=[pk, "bias"], w=[ok])
                else:
                    s.I("dve", lambda e: e.tensor_scalar(out=ot[:, :n], in0=p[:, :n], scalar1=bias[:, j:j + 1],
                                                         scalar2=None, op0=ALU.add), r=[pk, "bias"], w=[ok])
                s.dma("sp", y[j * 128:(j + 1) * 128, c0:c0 + n], ot[:, :n], r=[ok])
        s.finish()
    return nc


def pm(v):
    return np.ascontiguousarray(np.asarray(v, np.float32).reshape(-1, 128).T)


def make_mods(mod, norm_g, layer, b, which):
    ml = mod[layer, b].reshape(6, D)
    mc = mod[layer, 4].reshape(6, D)
    sh, sc = (0, 1) if which == 0 else (3, 4)
    return np.ascontiguousarray(np.stack([pm(norm_g[layer, which]), pm(ml[sc]), pm(ml[sh]), pm(mc[sc]), pm(mc[sh])], 1))


TILES2 = [(0, 128, 1)] + [(128 + 256 * i, 256, 0) for i in range(8)]
NEG = -1.0e30


def build_c1():
    nc = new_nc()
    x = din(nc, "x", [D, TOK])
    yin = din(nc, "yin", [D, TOK])
    wout_d = din(nc, "wout", [D, D])
    bout_d = din(nc, "bout", [128, 8])
    mods_d = din(nc, "mods", [128, 5, 8])
    gates_d = din(nc, "gates", [128, 2, 8])
    wq_d = din(nc, "wq", [D, 2048])
    keys_d = din(nc, "keys", [128, 16, 128])
    ones_d = din(nc, "ones", [128, 128])
    x1_d = dout(nc, "x1", [D, TOK])
    h2b_d = dout(nc, "h2b", [D, TOK], BF16)
    sc_d = dout(nc, "sc", [TOK, 2048])
    tc_d = dout(nc, "tc", [TOK, 16])
    xr = x.rearrange("(k p) t -> p k t", p=128)
    yr = yin.rearrange("(k p) t -> p k t", p=128)
    x1r = x1_d.rearrange("(k p) t -> p k t", p=128)
    h2r = h2b_d.rearrange("(k p) t -> p k t", p=128)
    with ExitStack() as es:
        s = Sched(nc, es)
        emit_consts(s)
        wout = s.sb("wout", [128, DC, D])
        wq = s.sb("wq", [128, DC, 2048])
        keys = s.sb("keys", [128, 16, 128])
        mods = s.sb("mods", [128, 5, 8])
        ab = s.sb("ab", [128, 2, 8])
        gates = s.sb("gates", [128, 2, 8])
        bout = s.sb("bout", [128, 8])
        ones = s.sb("ones", [128, 128])
        xt = s.sb("xt", [128, DC, 256])
        yt = s.sb("yt", [128, DC, 256])
        x1t = s.sb("x1t", [128, DC, 256])
        sq = s.sb("sq", [128, DC, 256])
        ht = s.sb("ht", [128, DC, 256])
        hb = s.sb("hb", [128, DC, 256], BF16)
        rstd = s.sb("rstd", [128, 256])
        qT = s.sb("qT", [128, 16, 256])
        ssb = s.sb("ssb", [128, 2048])
        stmp = s.sb("stmp", [128, 2048])
        v16 = s.sb("v16", [128, 16, 16])
        cand = s.sb("cand", [128, 8, 256])
        cand2 = s.sb("cand2", [128, 8, 256])
        c16 = s.sb("c16", [128, 8, 16])
        e16 = s.sb("e16", [128, 8, 16])
        zz = s.sb("zz", [128, 8])
        tcs = s.sb("tcs", [128, 16])
        pss = s.ps("pss", [128, 512])
        pmm = [s.ps("pmm%d" % i, [128, 512]) for i in range(2)]
        psc = s.ps("psc", [128, 2048])
        for nm, t, d in (("mods", mods, mods_d), ("gates", gates, gates_d), ("bout", bout, bout_d),
                         ("ones", ones, ones_d), ("keys", keys, keys_d)):
            s.dma("sp", t[:], d, w=[nm])
        for k in range(DC):
            s.dma("pool" if k % 2 else "sp", wout[:, k, :], wout_d[k * 128:(k + 1) * 128, :], w=["wout%d" % k])
        for k in range(DC):
            s.dma("pool" if k % 2 else "sp", wq[:, k, :], wq_d[k * 128:(k + 1) * 128, :], w=["wq%d" % k])
        emit_modprep(s, mods, ab)
        for ti, (c0, n, isc) in enumerate(TILES2):
            gi = 1 if isc else 0
            s.dma("sp", xt[:, :, :n], xr[:, :, c0:c0 + n], w=["xt"])
            s.dma("sp", yt[:, :, :n], yr[:, :, c0:c0 + n], w=["yt"])
            for j in range(DC):
                p, pk = pmm[j % 2], "pmm%d" % (j % 2)
                for k in range(DC):
                    s.I("pe", lambda e: e.matmul(p[:, :n], lhsT=wout[:, k, j * 128:(j + 1) * 128], rhs=yt[:, k, :n],
                                                 start=(k == 0), stop=(k == DC - 1)),
                        r=["wout%d" % k, "yt"], w=[pk], sig=(k == DC - 1))
                s.I("act", lambda e: e.activation(out=x1t[:, j, :n], in_=p[:, :n], func=AF.Identity,
                                                  bias=bout[:, j:j + 1]), r=[pk, "bout"], w=["x1t"])
                s.I("dve", lambda e: e.scalar_tensor_tensor(out=x1t[:, j, :n], in0=x1t[:, j, :n],
                                                            scalar=gates[:, gi, j:j + 1], in1=xt[:, j, :n],
                                                            op0=ALU.mult, op1=ALU.add),
                    r=["x1t", "gates", "xt"], w=["x1t"])
            s.dma("sp", x1r[:, :, c0:c0 + n], x1t[:, :, :n], r=["x1t"])
            emit_normmod(s, x1t, "x1t", n, isc, ht, "ht", sq, rstd, ones, ab, mods, pss, hb, "hb")
            s.dma("sp", h2r[:, :, c0:c0 + n], hb[:, :, :n], r=["hb"])
            for c in range(16):
                p, pk = pmm[c % 2], "pmm%d" % (c % 2)
                for k in range(DC):
                    s.I("pe", lambda e: e.matmul(p[:, :n], lhsT=wq[:, k, c * 128:(c + 1) * 128], rhs=ht[:, k, :n],
                                                 start=(k == 0), stop=(k == DC - 1)),
                        r=["wq%d" % k, "ht"], w=[pk], sig=(k == DC - 1))
                if c % 2 == 0:
                    s.I("act", lambda e: e.activation(out=qT[:, c, :n], in_=p[:, :n], func=AF.Copy), r=[pk], w=["qT"])
                else:
                    s.I("dve", lambda e: e.tensor_copy(out=qT[:, c, :n], in_=p[:, :n]), r=[pk], w=["qT"])
            for st in range(n // 128):
                t0 = c0 + st * 128
                for c in range(16):
                    s.I("pe", lambda e: e.matmul(psc[:, c * 128:(c + 1) * 128], lhsT=qT[:, c, st * 128:(st + 1) * 128],
                                                 rhs=keys[:, c, :], start=True, stop=True),
                        r=["qT", "keys"], w=["psc"], sig=(c == 15))
                s.I("act", lambda e: e.activation(out=ssb[:], in_=psc[:], func=AF.Copy), r=["psc"], w=["ssb"])
                s.dma("sp", sc_d[t0:t0 + 128, :], ssb[:], r=["ssb"])
                for g in range(16):
                    sl = slice(g * 128, (g + 1) * 128)
                    s.I("dve", lambda e: e.max(out=v16[:, g, 0:8], in_=ssb[:, sl]), r=["ssb"], w=["v16"])
                    s.I("dve", lambda e: e.match_replace(out=stmp[:, sl], in_to_replace=v16[:, g, 0:8],
                                                         in_values=ssb[:, sl], imm_value=NEG),
                        r=["ssb", "v16"], w=["stmp"])
                    s.I("dve", lambda e: e.max(out=v16[:, g, 8:16], in_=stmp[:, sl]), r=["stmp"], w=["v16"])
                s.I("pool", lambda e: e.tensor_tensor(out=AP(cand, 0, [[256, 8], [16, 16], [1, 16]]),
                                                      in0=AP(v16, 0, [[32, 8], [1, 16], [0, 16]]),
                                                      in1=AP(v16, 16, [[32, 8], [0, 16], [1, 16]]), op=ALU.add),
                    r=["v16"], w=["cand"])
                for h in range(8):
                    s.I("dve", lambda e: e.max(out=c16[:, h, 0:8], in_=cand[:, h, :]), r=["cand"], w=["c16"])
                    s.I("dve", lambda e: e.match_replace(out=cand2[:, h, :], in_to_replace=c16[:, h, 0:8],
                                                         in_values=cand[:, h, :], imm_value=NEG),
                        r=["cand", "c16"], w=["cand2"])
                    s.I("dve", lambda e: e.max(out=c16[:, h, 8:16], in_=cand2[:, h, :]), r=["cand2"], w=["c16"])
                s.I("dve", lambda e: e.tensor_tensor(out=e16[:], in0=c16[:], in1=AP(c16, 0, [[16, 8], [0, 16]]),
                                                     op=ALU.subtract), r=["c16"], w=["e16"])
                s.I("act", lambda e: e.activation(out=e16[:], in_=e16[:], func=AF.Exp), r=["e16"], w=["e16"])
                s.I("dve", lambda e: e.tensor_reduce(out=zz[:], in_=e16[:], axis=AX.X, op=ALU.add), r=["e16"], w=["zz"])
                s.I("act", lambda e: e.activation(out=zz[:], in_=zz[:], func=AF.Ln), r=["zz"], w=["zz"])
                s.I("dve", lambda e: e.tensor_tensor(out=tcs[:, 8:16], in0=zz[:], in1=AP(c16, 0, [[16, 8]]), op=ALU.add),
                    r=["zz", "c16"], w=["tcs"])
                s.I("dve", lambda e: e.tensor_copy(out=tcs[:, 0:8], in_=AP(c16, 15, [[16, 8]])), r=["c16"], w=["tcs"])
                s.dma("sp", tc_d[t0:t0 + 128, :], tcs[:], r=["tcs"])
        s.finish()
    return nc


def build_wcast():
    nc = new_nc()
    uf = din(nc, "uf", [16, 128, 1024])
    vf = din(nc, "vf", [16, 128, 1024])
    ub = dout(nc, "ub", [16, 128, 1024], BF16)
    vb = dout(nc, "vb", [16, 128, 1024], BF16)
    with ExitStack() as es:
        s = Sched(nc, es)
        fin = [s.sb("fin%d" % i, [128, 4, 1024]) for i in range(3)]
        fo = [s.sb("fo%d" % i, [128, 4, 1024], BF16) for i in range(3)]
        engs = ["dve", "act", "pool"]
        i = 0
        for src, dst in ((uf, ub), (vf, vb)):
            for g in range(4):
                a, ak = fin[i % 3], "fin%d" % (i % 3)
                o, ok = fo[i % 3], "fo%d" % (i % 3)
                s.dma("sp", a[:], src[g * 4:(g + 1) * 4].rearrange("c p f -> p c f"), w=[ak])
                en = engs[i % 3]
                if en == "act":
                    s.I("act", lambda e: e.activation(out=o[:], in_=a[:], func=AF.Copy), r=[ak], w=[ok])
                else:
                    s.I(en, lambda e: e.tensor_copy(out=o[:], in_=a[:]), r=[ak], w=[ok])
                s.dma("sp", dst[g * 4:(g + 1) * 4].rearrange("c p f -> p c f"), o[:], r=[ok])
                i += 1
        s.finish()
    return nc


def build_c2(gelu_func=None, NQ=8, tiles=None, do_g=True, do_e=True):
    nc = new_nc()
    gelu_func = gelu_func or AF.Gelu_apprx_tanh
    x1_d = din(nc, "x1", [D, TOK])
    h2b_d = din(nc, "h2b", [D, TOK], BF16)
    sc_d = din(nc, "sc", [TOK, 2048])
    tc_d = din(nc, "tc", [TOK, 16])
    g2_d = din(nc, "g2", [128, 2, 8])
    ut_d = din(nc, "ut", [128, 128, 1024], BF16)
    vb_d = din(nc, "vb", [128, 128, 1024], BF16)
    identb_d = din(nc, "identb", [128, 128], BF16)
    identf_d = din(nc, "identf", [128, 128])
    x2_d = dout(nc, "x2", [D, TOK])
    x1r = x1_d.rearrange("(k p) t -> p k t", p=128)
    h2r = h2b_d.rearrange("(k p) t -> p k t", p=128)
    x2r = x2_d.rearrange("(k p) t -> p k t", p=128)
    IQ = 128 // NQ
    with ExitStack() as es:
        s = Sched(nc, es)
        g2 = s.sb("g2", [128, 2, 8])
        identb = s.sb("identb", [128, 128], BF16)
        identf = s.sb("identf", [128, 128])
        st_ = s.sb("st", [128, 2048])
        tcs = s.sb("tcs", [128, 16])
        negc = s.sb("negc", [128, 8])
        Sb = [s.sb("S%d" % i, [128, IQ * 128]) for i in range(2)]
        Eb = [s.sb("E%d" % i, [128, IQ * 128], BF16) for i in range(2)]
        Gh = s.sb("Gh", [128, IQ * 128], BF16)
        Gb = [s.sb("G%d" % i, [128, IQ * 128], BF16) for i in range(2)]
        GT = s.sb("GT", [128, 128, 256], BF16)
        UG = [s.sb("UG%d" % i, [128, 4, 1024], BF16) for i in range(2)]
        VG = [s.sb("VG%d" % i, [128, 4, 1024], BF16) for i in range(2)]
        hbt = s.sb("hbt", [128, DC, 256], BF16)
        x1t = s.sb("x1t", [128, DC, 256])
        x2t = s.sb("x2t", [128, DC, 256])
        geb = [s.sb("ge%d" % i, [128, 256], BF16) for i in range(2)]
        ptb = [s.sb("pt%d" % i, [128, 256], BF16) for i in range(2)]
        osb = s.sb("osb", [128, 1024])
        pact = [s.ps("pa%d" % i, [128, 512]) for i in range(2)]
        po = [s.ps("po%d" % i, [128, 512]) for i in range(4)]
        pT = s.ps("pT", [128, 1024], BF16)
        for nm, t, d in (("g2", g2, g2_d), ("identb", identb, identb_d), ("identf", identf, identf_d)):
            s.dma("sp", t[:], d, w=[nm])
        gcount = 0
        for ti, (c0, n, isc) in enumerate(tiles or TILES2):
            nst = n // 128
            gi = 1 if isc else 0
            s.dma("sp", hbt[:, :, :n], h2r[:, :, c0:c0 + n], w=["hbt"])
            s.dma("sp", x1t[:, :, :n], x1r[:, :, c0:c0 + n], w=["x1t"])
            for st in range(nst if do_g else 0):
                t0 = c0 + st * 128
                s.dma("sp", st_[:], sc_d[t0:t0 + 128, :], w=["st"])
                s.dma("sp", tcs[:], tc_d[t0:t0 + 128, :], w=["tcs"])
                s.I("dve", lambda e: e.tensor_scalar(out=negc[:], in0=tcs[:, 8:16], scalar1=-1.0, scalar2=None,
                                                     op0=ALU.mult), r=["tcs"], w=["negc"])
                for q in range(NQ):
                    G, gk = Gb[q % 2], "G%d" % (q % 2)
                    for h in range(8):
                        S, sk = Sb[gcount % 2], "S%d" % (gcount % 2)
                        E, ek = Eb[gcount % 2], "E%d" % (gcount % 2)
                        gcount += 1
                        s.I("pool", lambda e: e.tensor_tensor(
                            out=AP(S, 0, [[128, IQ], [1, 128]]),
                            in0=AP(st_, (2 * h) * 128 + q * IQ, [[1, IQ], [0, 128]]),
                            in1=AP(st_, (2 * h + 1) * 128, [[0, IQ], [1, 128]]), op=ALU.add),
                            r=["st"], w=[sk])
                        s.I("act", lambda e: e.activation(out=E[:], in_=S[:], func=AF.Exp, bias=negc[:, h:h + 1]),
                            r=[sk, "negc"], w=[ek])
                        if h == 0:
                            s.I("dve", lambda e: e.scalar_tensor_tensor(out=G[:], in0=S[:], scalar=tcs[:, h:h + 1],
                                                                        in1=E[:], op0=ALU.is_ge, op1=ALU.mult),
                                r=[sk, ek, "tcs"], w=[gk])
                        else:
                            s.I("dve", lambda e: e.scalar_tensor_tensor(out=Gh[:], in0=S[:], scalar=tcs[:, h:h + 1],
                                                                        in1=E[:], op0=ALU.is_ge, op1=ALU.mult),
                                r=[sk, ek, "tcs"], w=["Gh"])
                            s.I("dve", lambda e: e.tensor_tensor(out=G[:], in0=G[:], in1=Gh[:], op=ALU.add),
                                r=[gk, "Gh"], w=[gk])
                    for i0 in range(0, IQ, 8):
                        for il in range(i0, i0 + 8):
                            s.I("pe", lambda e: e.transpose(out=pT[:, (il - i0) * 128:(il - i0 + 1) * 128],
                                                            in_=G[:, il * 128:(il + 1) * 128], identity=identb[:]),
                                r=[gk, "identb"], w=["pT"], sig=(il == i0 + 7))
                        ch0 = q * IQ + i0
                        dst = AP(GT, ch0 * 256 + st * 128, [[256, 8], [1, 128]])
                        src = AP(pT, 0, [[128, 8], [1, 128]])
                        s.I("act", lambda e: e.activation(out=dst, in_=src, func=AF.Copy), r=["pT"], w=["GT"])
            if not do_g:
                s.I("pool", lambda e: e.memset(GT[:], 0.0), w=["GT"])
            for c in range(128 if do_e else 0):
                gsel = (c // 4) % 2
                if c % 4 == 0:
                    s.dma("sp", UG[gsel][:], ut_d[c:c + 4].rearrange("c p f -> p c f"), w=["UG%d" % gsel])
                    s.dma("pool", VG[gsel][:], vb_d[c:c + 4].rearrange("c p f -> p c f"), w=["VG%d" % gsel])
                U, V = UG[gsel], VG[gsel]
                pa, pak = pact[c % 2], "pa%d" % (c % 2)
                for k in range(DC):
                    s.I("pe", lambda e: e.matmul(pa[:, :n], lhsT=U[:, c % 4, k * 128:(k + 1) * 128], rhs=hbt[:, k, :n],
                                                 start=(k == 0), stop=(k == DC - 1)),
                        r=["UG%d" % gsel, "hbt"], w=[pak], sig=(k == DC - 1))
                ge, gek = geb[c % 2], "ge%d" % (c % 2)
                pt, ptk = ptb[c % 2], "pt%d" % (c % 2)
                s.I("act", lambda e: e.activation(out=ge[:, :n], in_=pa[:, :n], func=gelu_func), r=[pak], w=[gek])
                s.I("dve", lambda e: e.tensor_tensor(out=pt[:, :n], in0=ge[:, :n], in1=GT[:, c, :n], op=ALU.mult),
                    r=[gek, "GT"], w=[ptk])
                for st in range(nst):
                    for hf in range(2):
                        pidx = st * 2 + hf
                        s.I("pe", lambda e: e.matmul(po[pidx][:], lhsT=pt[:, st * 128:(st + 1) * 128],
                                                     rhs=V[:, c % 4, hf * 512:(hf + 1) * 512],
                                                     start=(c == 0), stop=(c == 127)),
                            r=[ptk, "VG%d" % gsel], w=["po%d" % pidx], sig=(st == nst - 1 and hf == 1))
            for st in range(nst):
                s.I("act", lambda e: e.activation(out=osb[:, 0:512], in_=po[st * 2][:], func=AF.Copy),
                    r=["po%d" % (st * 2)], w=["osb"])
                s.I("dve", lambda e: e.tensor_copy(out=osb[:, 512:1024], in_=po[st * 2 + 1][:]),
                    r=["po%d" % (st * 2 + 1)], w=["osb"])
                for k in range(DC):
                    pf, pfk = pact[k % 2], "pa%d" % (k % 2)
                    s.I("pe", lambda e: e.transpose(out=pf[:, 0:128], in_=osb[:, k * 128:(k + 1) * 128], identity=identf[:]),
                        r=["osb", "identf"], w=[pfk])
                    s.I("dve", lambda e: e.scalar_tensor_tensor(out=x2t[:, k, st * 128:(st + 1) * 128], in0=pf[:, 0:128],
                                                                scalar=g2[:, gi, k:k + 1],
                                                                in1=x1t[:, k, st * 128:(st + 1) * 128],
                                                                op0=ALU.mult, op1=ALU.add),
                        r=[pfk, "g2", "x1t"], w=["x2t"])
            s.dma("sp", x2r[:, :, c0:c0 + n], x2t[:, :, :n], r=["x2t"])
        s.finish()
    return nc


NT = LC + L
NBLK = NT // 128


def build_attn():
    nc = new_nc()
    q_d = din(nc, "q", [8, 64, NT])
    k_d = din(nc, "k", [2, 64, NT])
    v_d = din(nc, "v", [2, NT, 64])
    gn_d = din(nc, "gn", [64, 2])
    cos_d = din(nc, "cos", [64, NT])
    sin_d = din(nc, "sin", [64, NT])
    rot_d = din(nc, "rot", [64, 64])
    ones_d = din(nc, "ones", [64, 64])
    mlo_d = din(nc, "mlo", [128, 128])
    mhi_d = din(nc, "mhi", [128, 128])
    sink_d = din(nc, "sink", [128, 8])
    o_d = dout(nc, "o", [NT, 512])
    with ExitStack() as es:
        s = Sched(nc, es)
        eps64 = s.sb("eps64", [64, 1])
        s.I("dve", lambda e: e.memset(eps64[:], EPS), w=["eps64"])
        cosT = s.sb("cosT", [64, NT])
        sinT = s.sb("sinT", [64, NT])
        rot = s.sb("rot", [64, 64])
        ones = s.sb("ones", [64, 64])
        gn = s.sb("gn", [64, 2])
        mlo = s.sb("mlo", [128, 128])
        mhi = s.sb("mhi", [128, 128])
        esink = s.sb("esink", [128, 8])
        qt = s.sb("qt", [64, 4, NT])
        kt = s.sb("kt", [64, NT])
        vt = s.sb("vt", [128, NBLK, 65])
        sq = s.sb("sq", [64, 512])
        rstd = s.sb("rstd", [64, 512])
        xn = s.sb("xn", [64, 512])
        t1 = s.sb("t1", [64, 512])
        t2 = s.sb("t2", [64, 512])
        E = s.sb("E", [128, 5, 512])
        den = s.sb("den", [128, 4])
        osb = [s.sb("osb%d" % i, [128, 4, 64]) for i in range(2)]
        pn = s.ps("pn", [64, 512])
        pr = s.ps("pr", [64, 512])
        psS = s.ps("psS", [128, 5 * 512])
        psO = s.ps("psO", [128, 512])
        for nm, t, d in (("cosT", cosT, cos_d), ("sinT", sinT, sin_d), ("rot", rot, rot_d), ("ones", ones, ones_d),
                         ("gn", gn, gn_d), ("mlo", mlo, mlo_d), ("mhi", mhi, mhi_d), ("esink", esink, sink_d)):
            s.dma("sp", t[:], d, w=[nm])
        s.I("act", lambda e: e.activation(out=esink[:], in_=esink[:], func=AF.Exp), r=["esink"], w=["esink"])
        s.I("dve", lambda e: e.memset(vt[:, :, 64:65], 1.0), w=["vt1"])
        cblocks = [(c0, min(512, NT - c0)) for c0 in range(0, NT, 512)]
        for j in range(2):
            s.dma("sp", kt[:], k_d[j], w=["kt"])
            for g in range(4):
                s.dma("sp", qt[:, g, :], q_d[j * 4 + g], w=["qt"])
            s.dma("sp", vt[:, :, 0:64], v_d[j].rearrange("(n p) d -> p n d", p=128), w=["vt"])
            for hi in range(5):
                gcol = 1 if hi == 0 else 0
                for (c0, n) in cblocks:
                    if hi == 0:
                        xs = kt[:, c0:c0 + n]
                        xk = "kt"
                    else:
                        xs = qt[:, hi - 1, c0:c0 + n]
                        xk = "qt"
                    s.I("act", lambda e: e.activation(out=sq[:, :n], in_=xs, func=AF.Square), r=[xk], w=["sq"])
                    s.I("pe", lambda e: e.matmul(pn[:, :n], lhsT=ones[:], rhs=sq[:, :n], start=True, stop=True),
                        r=["sq", "ones"], w=["pn"])
                    s.I("act", lambda e: e.activation(out=rstd[:, :n], in_=pn[:, :n], func=AF.Sqrt, scale=1.0 / 64,
                                                      bias=eps64[:]), r=["pn", "eps64"], w=["rstd"])
                    s.I("dve", lambda e: e.reciprocal(out=rstd[:, :n], in_=rstd[:, :n]), r=["rstd"], w=["rstd"])
                    s.I("dve", lambda e: e.scalar_tensor_tensor(out=xn[:, :n], in0=xs, scalar=gn[:, gcol:gcol + 1],
                                                                in1=rstd[:, :n], op0=ALU.mult, op1=ALU.mult),
                        r=[xk, "gn", "rstd"], w=["xn"])
                    s.I("pe", lambda e: e.matmul(pr[:, :n], lhsT=rot[:], rhs=xn[:, :n], start=True, stop=True),
                        r=["xn", "rot"], w=["pr"])
                    s.I("dve", lambda e: e.tensor_tensor(out=t1[:, :n], in0=xn[:, :n], in1=cosT[:, c0:c0 + n], op=ALU.mult),
                        r=["xn", "cosT"], w=["t1"])
                    s.I("dve", lambda e: e.tensor_tensor(out=t2[:, :n], in0=pr[:, :n], in1=sinT[:, c0:c0 + n], op=ALU.mult),
                        r=["pr", "sinT"], w=["t2"])
                    s.I("dve", lambda e: e.tensor_tensor(out=xs, in0=t1[:, :n], in1=t2[:, :n], op=ALU.add),
                        r=["t1", "t2"], w=[xk])
            for qb in range(NBLK):
                if qb < 2:
                    kbs = [(0, None), (1, None)]
                else:
                    kbs = []
                    if qb - 1 >= 2:
                        kbs.append((qb - 1, mlo))
                    kbs.append((qb, None))
                    if qb + 1 < NBLK:
                        kbs.append((qb + 1, mhi))
                    kbs += [(0, None), (1, None)]
                nk = len(kbs)
                qcols = AP(qt, qb * 128, [[NT, 4], [1, 128]], nparts=64)
                for si, (kb, msk) in enumerate(kbs):
                    s.I("pe", lambda e: e.matmul(psS[:, si * 512:(si + 1) * 512], lhsT=kt[:, kb * 128:(kb + 1) * 128],
                                                 rhs=qcols, start=True, stop=True),
                        r=["kt", "qt"], w=["psS"], sig=(si == nk - 1))
                s.I("act", lambda e: e.activation(out=E[:, 0:nk, :], in_=psS[:, 0:nk * 512], func=AF.Exp, scale=0.125),
                    r=["psS"], w=["E"])
                for si, (kb, msk) in enumerate(kbs):
                    if msk is not None:
                        mk = "mlo" if msk is mlo else "mhi"
                        s.I("dve", lambda e: e.tensor_tensor(out=AP(E, si * 512, [[128, 4], [1, 128]]),
                                                             in0=AP(E, si * 512, [[128, 4], [1, 128]]),
                                                             in1=AP(msk, 0, [[0, 4], [1, 128]]), op=ALU.mult),
                            r=["E", mk], w=["E"])
                for g in range(4):
                    for si, (kb, msk) in enumerate(kbs):
                        s.I("pe", lambda e: e.matmul(psO[:, g * 65:(g + 1) * 65], lhsT=E[:, si, g * 128:(g + 1) * 128],
                                                     rhs=vt[:, kb, :], start=(si == 0), stop=(si == nk - 1)),
                            r=["E", "vt", "vt1"], w=["psO"], sig=(g == 3 and si == nk - 1))
                ob, obk = osb[qb % 2], "osb%d" % (qb % 2)
                s.I("dve", lambda e: e.tensor_tensor(out=den[:], in0=AP(psO, 64, [[65, 4]]), in1=esink[:, j * 4:(j + 1) * 4],
                                                     op=ALU.add), r=["psO", "esink"], w=["den"])
                s.I("dve", lambda e: e.reciprocal(out=den[:], in_=den[:]), r=["den"], w=["den"])
                s.I("dve", lambda e: e.tensor_tensor(out=ob[:], in0=AP(psO, 0, [[65, 4], [1, 64]]),
                                                     in1=AP(den, 0, [[1, 4], [0, 64]]), op=ALU.mult),
                    r=["psO", "den"], w=[obk])
                s.dma("sp", o_d[qb * 128:(qb + 1) * 128, j * 256:(j + 1) * 256], ob[:], r=[obk])
        s.finish()
    return nc


def rope_tables():
    t = np.arange(L)
    row = (t // 64).astype(np.float32)
    col = (t % 64).astype(np.float32)
    freqs = (np.float32(10000.0) ** (-np.arange(16, dtype=np.float32) / np.float32(16))).astype(np.float32)
    ang = np.stack([row[:, None] * freqs, col[:, None] * freqs], 1)
    cos = np.cos(ang).astype(np.float32)
    sin = np.sin(ang).astype(np.float32)
    cosT = np.ones((64, NT), np.float32)
    sinT = np.zeros((64, NT), np.float32)
    for half in range(2):
        for pair in range(2):
            d0 = half * 32 + pair * 16
            cosT[d0:d0 + 16, LC:] = cos[:, half, :].T
            sinT[d0:d0 + 16, LC:] = sin[:, half, :].T
    rot = np.zeros((64, 64), np.float32)
    for half in range(2):
        for f in range(16):
            d1 = half * 32 + f
            d2 = d1 + 16
            rot[d2, d1] = -1.0
            rot[d1, d2] = 1.0
    return cosT, sinT, rot


def attn_consts():
    cosT, sinT, rot = rope_tables()
    kk = np.arange(128)[:, None]
    qq = np.arange(128)[None, :]
    return dict(cos=cosT, sin=sinT, rot=rot, ones=np.ones((64, 64), np.float32),
                mlo=(kk >= qq).astype(np.float32), mhi=(kk <= qq).astype(np.float32))


def build_lru():
    nc = new_nc()
    g_d = din(nc, "g", [512, NT])
    u_d = din(nc, "u", [512, NT])
    cw_d = din(nc, "cw", [128, 4, 4])
    cb_d = din(nc, "cb", [128, 4])
    wa_d = din(nc, "wa", [128, 2, 4, 128])
    wx_d = din(nc, "wx", [128, 2, 4, 128])
    ba_d = din(nc, "ba", [128, 2, 4])
    bx_d = din(nc, "bx", [128, 2, 4])
    lam_d = din(nc, "lam", [128, 2, 4])
    z_d = dout(nc, "z", [512, NT])
    with ExitStack() as es:
        s = Sched(nc, es)
        one = s.sb("one", [128, 1])
        s.I("dve", lambda e: e.memset(one[:], 1.0), w=["one"])
        cw = s.sb("cw", [128, 4, 4])
        cb = s.sb("cb", [128, 4])
        wa = s.sb("wa", [128, 2, 4, 128])
        wx = s.sb("wx", [128, 2, 4, 128])
        ba = s.sb("ba", [128, 2, 4])
        bx = s.sb("bx", [128, 2, 4])
        lam = s.sb("lam", [128, 2, 4])
        m8 = s.sb("m8", [128, 2, 4])
        m16 = s.sb("m16", [128, 2, 4])
        gt = s.sb("gt", [128, NT])
        upc = s.sb("upc", [128, LC + 3])
        upl = s.sb("upl", [128, L + 3])
        u = s.sb("u", [128, NT])
        rr = s.sb("rr", [128, NT])
        ig = s.sb("ig", [128, NT])
        aa = s.sb("aa", [128, NT])
        bt = s.sb("bt", [128, NT])
        hs = s.sb("hs", [128, NT])
        hsum = s.sb("hsum", [128, NT])
        pps = [s.ps("pp%d" % i, [128, 512]) for i in range(4)]
        for nm, t, d in (("cw", cw, cw_d), ("cb", cb, cb_d), ("wa", wa, wa_d), ("wx", wx, wx_d), ("ba", ba, ba_d),
                         ("bx", bx, bx_d), ("lam", lam, lam_d)):
            s.dma("sp", t[:], d, w=[nm])
        s.I("act", lambda e: e.activation(out=m8[:], in_=lam[:], func=AF.Exp, scale=-1.0), r=["lam"], w=["m8"])
        s.I("act", lambda e: e.activation(out=m8[:], in_=m8[:], func=AF.Ln, bias=one[:]), r=["m8", "one"], w=["m8"])
        s.I("dve", lambda e: e.tensor_scalar(out=m16[:], in0=m8[:], scalar1=-16.0, scalar2=None, op0=ALU.mult),
            r=["m8"], w=["m16"])
        s.I("dve", lambda e: e.tensor_scalar(out=m8[:], in0=m8[:], scalar1=-8.0, scalar2=None, op0=ALU.mult),
            r=["m8", "m16"], w=["m8"])
        s.I("pool", lambda e: e.memset(upc[:], 0.0), w=["upc"])
        s.I("pool", lambda e: e.memset(upl[:], 0.0), w=["upl"])
        blocks = [(0, 256)] + [(256 + 512 * i, 512) for i in range(8)]
        segs = [(0, LC, upc, "upc"), (LC, L, upl, "upl")]
        pi = 0
        for c in range(4):
            rows = slice(c * 128, (c + 1) * 128)
            s.dma("sp", gt[:], g_d[rows, :], w=["gt"])
            s.dma("sp", upc[:, 2:2 + LC], u_d[rows, 0:LC], w=["upc"])
            s.dma("sp", upl[:, 2:2 + L], u_d[rows, LC:NT], w=["upl"])
            s.I("act", lambda e: e.activation(out=gt[:], in_=gt[:], func=AF.Gelu_apprx_tanh), r=["gt"], w=["gt"])
            for (o0, ln, up, upk) in segs:
                s.I("dve", lambda e: e.tensor_scalar(out=u[:, o0:o0 + ln], in0=up[:, 0:ln], scalar1=cw[:, c, 0:1],
                                                     scalar2=cb[:, c:c + 1], op0=ALU.mult, op1=ALU.add),
                    r=[upk, "cw", "cb"], w=["u"])
                for k in range(1, 4):
                    s.I("dve", lambda e: e.scalar_tensor_tensor(out=u[:, o0:o0 + ln], in0=up[:, k:k + ln],
                                                                scalar=cw[:, c, k:k + 1], in1=u[:, o0:o0 + ln],
                                                                op0=ALU.mult, op1=ALU.add),
                        r=[upk, "cw", "u"], w=["u"])
            for d in range(2):
                for (c0, n) in blocks:
                    pa_, pak = pps[pi % 4], "pp%d" % (pi % 4)
                    pi += 1
                    px_, pxk = pps[pi % 4], "pp%d" % (pi % 4)
                    pi += 1
                    s.I("pe", lambda e: e.matmul(pa_[:, :n], lhsT=wa[:, d, c, :], rhs=u[:, c0:c0 + n], start=True, stop=True),
                        r=["wa", "u"], w=[pak])
                    s.I("pe", lambda e: e.matmul(px_[:, :n], lhsT=wx[:, d, c, :], rhs=u[:, c0:c0 + n], start=True, stop=True),
                        r=["wx", "u"], w=[pxk])
                    s.I("act", lambda e: e.activation(out=rr[:, c0:c0 + n], in_=pa_[:, :n], func=AF.Sigmoid,
                                                      bias=ba[:, d, c:c + 1]), r=[pak, "ba"], w=["rr"])
                    s.I("act", lambda e: e.activation(out=ig[:, c0:c0 + n], in_=px_[:, :n], func=AF.Sigmoid,
                                                      bias=bx[:, d, c:c + 1]), r=[pxk, "bx"], w=["ig"])
                s.I("act", lambda e: e.activation(out=aa[:], in_=rr[:], func=AF.Exp, scale=m8[:, d, c:c + 1]),
                    r=["rr", "m8"], w=["aa"])
                s.I("act", lambda e: e.activation(out=bt[:], in_=rr[:], func=AF.Exp, scale=m16[:, d, c:c + 1]),
                    r=["rr", "m16"], w=["bt"])
                s.I("act", lambda e: e.activation(out=bt[:], in_=bt[:], func=AF.Sqrt, scale=-1.0, bias=one[:]),
                    r=["bt", "one"], w=["bt"])
                s.I("dve", lambda e: e.tensor_tensor(out=ig[:], in0=ig[:], in1=u[:], op=ALU.mult), r=["ig", "u"], w=["ig"])
                s.I("dve", lambda e: e.tensor_tensor(out=bt[:], in0=bt[:], in1=ig[:], op=ALU.mult), r=["bt", "ig"], w=["bt"])
                if d == 0:
                    s.I("dve", lambda e: e.tensor_tensor_scan(out=hs[:, 0:LC], data0=aa[:, 0:LC], data1=bt[:, 0:LC],
                                                              initial=0.0, op0=ALU.mult, op1=ALU.add),
                        r=["aa", "bt"], w=["hs"])
                    s.I("dve", lambda e: e.tensor_tensor_scan(out=hs[:, LC:NT], data0=aa[:, LC:NT], data1=bt[:, LC:NT],
                                                              initial=hs[:, LC - 1:LC], op0=ALU.mult, op1=ALU.add),
                        r=["aa", "bt", "hs"], w=["hs"])
                    s.I("pool", lambda e: e.tensor_copy(out=hsum[:], in_=hs[:]), r=["hs"], w=["hsum"])
                else:
                    rc = lambda t: AP(t, LC - 1, [[-1, LC]])
                    rl = lambda t: AP(t, NT - 1, [[-1, L]])
                    s.I("dve", lambda e: e.tensor_tensor_scan(out=rc(hs), data0=rc(aa), data1=rc(bt),
                                                              initial=0.0, op0=ALU.mult, op1=ALU.add),
                        r=["aa", "bt"], w=["hs"])
                    s.I("dve", lambda e: e.tensor_tensor_scan(out=rl(hs), data0=rl(aa), data1=rl(bt),
                                                              initial=hs[:, 0:1], op0=ALU.mult, op1=ALU.add),
                        r=["aa", "bt", "hs"], w=["hs"])
                    s.I("pool", lambda e: e.tensor_tensor(out=hsum[:], in0=hsum[:], in1=hs[:], op=ALU.add),
                        r=["hs", "hsum"], w=["hsum"])
            s.I("dve", lambda e: e.tensor_tensor(out=hsum[:], in0=hsum[:], in1=gt[:], op=ALU.mult), r=["hsum", "gt"], w=["hsum"])
            s.dma("sp", z_d[rows, :], hsum[:], r=["hsum"])
        s.finish()
    return nc


def lru_params(inp, s):
    ch = slice(512 * s, 512 * (s + 1))
    pl = lambda v: np.ascontiguousarray(np.asarray(v)[..., ch].reshape(v.shape[:-1] + (4, 128)))
    cw = np.ascontiguousarray(pl(inp["lru_conv_w"][0]).transpose(2, 1, 0))
    cb = np.ascontiguousarray(pl(inp["lru_conv_b"][0]).T)
    wa = np.ascontiguousarray(inp["lru_w_a"][0][:, 4 * s:4 * s + 4].transpose(2, 0, 1, 3))
    wx = np.ascontiguousarray(inp["lru_w_x"][0][:, 4 * s:4 * s + 4].transpose(2, 0, 1, 3))
    t3 = lambda v: np.ascontiguousarray(pl(v).transpose(2, 0, 1))
    return dict(cw=cw, cb=cb, wa=wa, wx=wx, ba=t3(inp["lru_b_a"][0]), bx=t3(inp["lru_b_x"][0]), lam=t3(inp["lru_lambda"][0]))


def build_conf():
    nc = new_nc()
    HAL = 30
    vgc_d = din(nc, "vgc", [2048, 128 + HAL])
    vgl_d = din(nc, "vgl", [2048, 2048 + HAL])
    dw_d = din(nc, "dw", [128, 8, 31])
    dwb_d = din(nc, "dwb", [128, 8])
    lng_d = din(nc, "lng", [128, 8])
    lnb_d = din(nc, "lnb", [128, 8])
    ones_d = din(nc, "ones", [128, 128])
    z_d = dout(nc, "z", [D, TOK])
    with ExitStack() as es:
        s = Sched(nc, es)
        emit_consts(s)
        dw = s.sb("dw", [128, 8, 31])
        dwb = s.sb("dwb", [128, 8])
        lng = s.sb("lng", [128, 8])
        lnb = s.sb("lnb", [128, 8])
        ones = s.sb("ones", [128, 128])
        U = s.sb("U", [128, 8, 2048])
        val = [s.sb("val%d" % i, [128, 2048 + HAL]) for i in range(2)]
        gate = [s.sb("gate%d" % i, [128, 2048 + HAL]) for i in range(2)]
        sq = s.sb("sq", [128, 8, 512])
        mean = s.sb("mean", [128, 512])
        msq = s.sb("msq", [128, 512])
        rstd = s.sb("rstd", [128, 512])
        tt = [s.sb("tt%d" % i, [128, 512]) for i in range(2)]
        zo = [s.sb("zo%d" % i, [128, 512]) for i in range(2)]
        ps1 = s.ps("ps1", [128, 512])
        ps2 = s.ps("ps2", [128, 512])
        for nm, t, d in (("dw", dw, dw_d), ("dwb", dwb, dwb_d), ("lng", lng, lng_d), ("lnb", lnb, lnb_d), ("ones", ones, ones_d)):
            s.dma("sp", t[:], d, w=[nm])
        zi = 0
        for (src, Ls, ocol) in ((vgc_d, 128, 0), (vgl_d, 2048, 128)):
            W = Ls + HAL
            for k in range(8):
                v, vk = val[k % 2], "val%d" % (k % 2)
                g, gk = gate[k % 2], "gate%d" % (k % 2)
                s.dma("sp", v[:, :W], src[k * 128:(k + 1) * 128, :], w=[vk])
                s.dma("sp", g[:, :W], src[1024 + k * 128:1024 + (k + 1) * 128, :], w=[gk])
                s.I("act", lambda e: e.activation(out=g[:, :W], in_=g[:, :W], func=AF.Sigmoid), r=[gk], w=[gk])
                s.I("pool", lambda e: e.tensor_tensor(out=v[:, :W], in0=v[:, :W], in1=g[:, :W], op=ALU.mult), r=[vk, gk], w=[vk])
                s.I("dve", lambda e: e.tensor_scalar(out=U[:, k, :Ls], in0=v[:, 0:Ls], scalar1=dw[:, k, 0:1],
                                                     scalar2=dwb[:, k:k + 1], op0=ALU.mult, op1=ALU.add),
                    r=[vk, "dw", "dwb"], w=["U%d" % k])
                for j in range(1, 31):
                    s.I("dve", lambda e: e.scalar_tensor_tensor(out=U[:, k, :Ls], in0=v[:, j:j + Ls], scalar=dw[:, k, j:j + 1],
                                                                in1=U[:, k, :Ls], op0=ALU.mult, op1=ALU.add),
                        r=[vk, "dw", "U%d" % k], w=["U%d" % k])
            for c0 in range(0, Ls, 512):
                n = min(512, Ls - c0)
                uk = ["U%d" % k for k in range(8)]
                s.I("act", lambda e: e.activation(out=sq[:, :, :n], in_=U[:, :, c0:c0 + n], func=AF.Square), r=uk, w=["sq"])
                for k in range(8):
                    s.I("pe", lambda e: e.matmul(ps1[:, :n], lhsT=ones[:], rhs=U[:, k, c0:c0 + n], start=(k == 0), stop=(k == 7)),
                        r=["U%d" % k, "ones"], w=["ps1"], sig=(k == 7))
                for k in range(8):
                    s.I("pe", lambda e: e.matmul(ps2[:, :n], lhsT=ones[:], rhs=sq[:, k, :n], start=(k == 0), stop=(k == 7)),
                        r=["sq", "ones"], w=["ps2"], sig=(k == 7))
                s.I("act", lambda e: e.activation(out=mean[:, :n], in_=ps1[:, :n], func=AF.Copy, scale=1.0 / D), r=["ps1"], w=["mean"])
                s.I("dve", lambda e: e.tensor_tensor(out=msq[:, :n], in0=mean[:, :n], in1=mean[:, :n], op=ALU.mult),
                    r=["mean"], w=["msq"])
                s.I("dve", lambda e: e.scalar_tensor_tensor(out=rstd[:, :n], in0=ps2[:, :n], scalar=1.0 / D, in1=msq[:, :n],
                                                            op0=ALU.mult, op1=ALU.subtract), r=["ps2", "msq"], w=["rstd"])
                s.I("act", lambda e: e.activation(out=rstd[:, :n], in_=rstd[:, :n], func=AF.Sqrt, bias=s.epsb[:]),
                    r=["rstd", "epsb"], w=["rstd"])
                s.I("dve", lambda e: e.reciprocal(out=rstd[:, :n], in_=rstd[:, :n]), r=["rstd"], w=["rstd"])
                for k in range(8):
                    t_, tk = tt[k % 2], "tt%d" % (k % 2)
                    z_, zk = zo[zi % 2], "zo%d" % (zi % 2)
                    zi += 1
                    s.I("dve", lambda e: e.tensor_tensor(out=t_[:, :n], in0=U[:, k, c0:c0 + n], in1=mean[:, :n], op=ALU.subtract),
                        r=["U%d" % k, "mean"], w=[tk])
                    s.I("pool", lambda e: e.tensor_tensor(out=t_[:, :n], in0=t_[:, :n], in1=rstd[:, :n], op=ALU.mult),
                        r=[tk, "rstd"], w=[tk])
                    s.I("act", lambda e: e.activation(out=z_[:, :n], in_=t_[:, :n], func=AF.Silu, scale=lng[:, k:k + 1],
                                                      bias=lnb[:, k:k + 1]), r=[tk, "lng", "lnb"], w=[zk])
                    s.dma("sp", z_d[k * 128:(k + 1) * 128, ocol + c0:ocol + c0 + n], z_[:, :n], r=[zk])
        s.finish()
    return nc


_PROGS = {}


def _prog(name, fn, *a):
    key = (name,) + a
    if key not in _PROGS:
        _PROGS[key] = fn(*a)
    return _PROGS[key]


def _halo(seg, s, Ls):
    Lt = seg.shape[1]
    out = np.zeros((seg.shape[0], Ls + 30), np.float32)
    a, b = s * Ls - 15, (s + 1) * Ls + 15
    a2, b2 = max(a, 0), min(b, Lt)
    out[:, a2 - a:a2 - a + (b2 - a2)] = seg[:, a2:b2]
    return out


def kernel(**inp):
    import ml_dtypes
    inp = {k: np.asarray(v) for k, v in inp.items()}
    x, ctx, ng = inp["x"], inp["ctx"], inp["norm_g"]
    mod = run_ada(inp)
    ones128 = np.ones((128, 128), np.float32)
    identb = np.eye(128, dtype=ml_dtypes.bfloat16)
    identf = np.eye(128, dtype=np.float32)
    acst = attn_consts()
    xs = []
    for c in range(NCORES):
        b, h = c // 2, c % 2
        xs.append(np.ascontiguousarray(np.concatenate([ctx[b, h * 128:(h + 1) * 128], x[b, h * 2048:(h + 1) * 2048]], 0).T))
    for layer in range(4):
        kind, slot = layer % 3, layer // 3
        if kind == 0:
            W, bias = inp["attn_wqkv"][slot], np.zeros(1536, np.float32)
        elif kind == 1:
            W, bias = inp["lru_w_in"][0], np.zeros(2048, np.float32)
        else:
            W, bias = inp["conf_w_pw1"][0], inp["conf_b_pw1"][0]
        N = W.shape[1]
        W = np.ascontiguousarray(W)
        biasl = np.ascontiguousarray(bias.reshape(-1, 128).T)
        maps = [{"x": xs[c], "w": W, "mods": make_mods(mod, ng, layer, c // 2, 0), "bias": biasl, "ones": ones128}
                for c in range(NCORES)]
        ys = [r["y"] for r in run(_prog("pre", build_pre, N), maps)]
        yb = [np.concatenate([ys[2 * b][:, :128], ys[2 * b + 1][:, :128], ys[2 * b][:, 128:], ys[2 * b + 1][:, 128:]], 1)
              for b in range(B)]
        if kind == 0:
            maps = []
            for c in range(NCORES):
                b, s = c // 2, c % 2
                q = yb[b][0:1024].reshape(16, 64, NT)[8 * s:8 * s + 8]
                k = yb[b][1024:1280].reshape(4, 64, NT)[2 * s:2 * s + 2]
                v = yb[b][1280:1536].reshape(4, 64, NT)[2 * s:2 * s + 2].transpose(0, 2, 1)
                gn = np.stack([inp["attn_q_gain"][slot], inp["attn_k_gain"][slot]], 1)
                sink = np.broadcast_to(inp["attn_sink"][slot][8 * s:8 * s + 8][None, :], (128, 8))
                m = dict(q=np.ascontiguousarray(q), k=np.ascontiguousarray(k), v=np.ascontiguousarray(v),
                         gn=np.ascontiguousarray(gn), sink=np.ascontiguousarray(sink))
                m.update(acst)
                maps.append(m)
            os_ = [r["o"] for r in run(_prog("attn", build_attn), maps)]
            yin = []
            for c in range(NCORES):
                b, h = c // 2, c % 2
                ob = np.concatenate([os_[2 * b], os_[2 * b + 1]], 1)
                yin.append(np.ascontiguousarray(
                    np.concatenate([ob[h * 128:(h + 1) * 128], ob[LC + h * 2048:LC + (h + 1) * 2048]], 0).T))
            wout, bout = inp["attn_wo"][slot], np.zeros(D, np.float32)
        elif kind == 1:
            maps = []
            for c in range(NCORES):
                b, s = c // 2, c % 2
                m = dict(g=np.ascontiguousarray(yb[b][512 * s:512 * s + 512]),
                         u=np.ascontiguousarray(yb[b][1024 + 512 * s:1024 + 512 * s + 512]))
                m.update(lru_params(inp, s))
                maps.append(m)
            zs = [r["z"] for r in run(_prog("lru", build_lru), maps)]
            yin = []
            for c in range(NCORES):
                b, h = c // 2, c % 2
                zb = np.concatenate([zs[2 * b], zs[2 * b + 1]], 0)
                yin.append(np.ascontiguousarray(
                    np.concatenate([zb[:, h * 128:(h + 1) * 128], zb[:, LC + h * 2048:LC + (h + 1) * 2048]], 1)))
            wout, bout = inp["lru_w_out"][0], np.zeros(D, np.float32)
        else:
            cp = dict(dw=np.ascontiguousarray(inp["conf_dw_w"][0].reshape(31, 8, 128).transpose(2, 1, 0)),
                      dwb=pm(inp["conf_dw_b"][0]), lng=pm(inp["conf_ln_g"][0]), lnb=pm(inp["conf_ln_b"][0]), ones=ones128)
            maps = []
            for c in range(NCORES):
                b, s = c // 2, c % 2
                m = dict(vgc=_halo(yb[b][:, :LC], s, 128), vgl=_halo(yb[b][:, LC:], s, 2048))
                m.update(cp)
                maps.append(m)
            yin = [r["z"] for r in run(_prog("conf", build_conf), maps)]
            wout, bout = inp["conf_w_pw2"][0], inp["conf_b_pw2"][0]
        u, v = inp["peer_u"][layer], inp["peer_v"][layer]
        utf = np.ascontiguousarray(u.reshape(128, 128, 8, 128).transpose(0, 3, 2, 1).reshape(128, 128, 1024))
        vf = v.reshape(128, 128, 1024)
        res = run(_prog("wcast", build_wcast), [{"uf": utf[16 * c:16 * c + 16], "vf": vf[16 * c:16 * c + 16]}
                                                for c in range(NCORES)])
        ub = np.concatenate([np.asarray(r["ub"]) for r in res], 0)
        vb = np.concatenate([np.asarray(r["vb"]) for r in res], 0)
        del utf
        keys = np.zeros((128, 16, 128), np.float32)
        for h in range(8):
            keys[:, 2 * h, :] = inp["peer_keys1"][layer][h].T
            keys[:, 2 * h + 1, :] = inp["peer_keys2"][layer][h].T
        wq = np.ascontiguousarray(inp["peer_wq"][layer])
        wout = np.ascontiguousarray(wout)
        maps = []
        for c in range(NCORES):
            b = c // 2
            ml = mod[layer, b].reshape(6, D)
            mc = mod[layer, 4].reshape(6, D)
            maps.append({"x": xs[c], "yin": yin[c], "wout": wout, "bout": pm(bout), "mods": make_mods(mod, ng, layer, b, 1),
                         "gates": np.ascontiguousarray(np.stack([pm(ml[2]), pm(mc[2])], 1)), "wq": wq, "keys": keys,
                         "ones": ones128})
        r1 = run(_prog("c1", build_c1), maps)
        maps = []
        for c in range(NCORES):
            b = c // 2
            ml = mod[layer, b].reshape(6, D)
            mc = mod[layer, 4].reshape(6, D)
            maps.append({"x1": r1[c]["x1"], "h2b": r1[c]["h2b"], "sc": r1[c]["sc"], "tc": r1[c]["tc"],
                         "g2": np.ascontiguousarray(np.stack([pm(ml[5]), pm(mc[5])], 1)), "ut": ub, "vb": vb,
                         "identb": identb, "identf": identf})
        r2 = run(_prog("c2q", build_c2q), maps)
        xs = [np.asarray(r["x2"]) for r in r2]
    out = np.zeros((B, L, D), np.float32)
    for c in range(NCORES):
        b, h = c // 2, c % 2
        out[b, h * 2048:(h + 1) * 2048] = xs[c][:, 128:].T
    return out


def build_c2p(NQ=16, WG=2):
    nc = new_nc()
    gelu_func = AF.Gelu_apprx_tanh
    x1_d = din(nc, "x1", [D, TOK])
    h2b_d = din(nc, "h2b", [D, TOK], BF16)
    sc_d = din(nc, "sc", [TOK, 2048])
    tc_d = din(nc, "tc", [TOK, 16])
    g2_d = din(nc, "g2", [128, 2, 8])
    ut_d = din(nc, "ut", [128, 128, 1024], BF16)
    vb_d = din(nc, "vb", [128, 128, 1024], BF16)
    identb_d = din(nc, "identb", [128, 128], BF16)
    identf_d = din(nc, "identf", [128, 128])
    x2_d = dout(nc, "x2", [D, TOK])
    x1r = x1_d.rearrange("(k p) t -> p k t", p=128)
    h2r = h2b_d.rearrange("(k p) t -> p k t", p=128)
    x2r = x2_d.rearrange("(k p) t -> p k t", p=128)
    IQ = 128 // NQ
    with ExitStack() as es:
        s = Sched(nc, es)
        g2 = s.sb("g2", [128, 2, 8])
        identb = s.sb("identb", [128, 128], BF16)
        identf = s.sb("identf", [128, 128])
        st_ = s.sb("st", [128, 2048])
        tcs = s.sb("tcs", [128, 16])
        negc = s.sb("negc", [128, 8])
        Sb = [s.sb("S%d" % i, [128, IQ * 128]) for i in range(2)]
        Eb = [s.sb("E%d" % i, [128, IQ * 128], BF16) for i in range(2)]
        Gh = s.sb("Gh", [128, IQ * 128], BF16)
        Gb = [s.sb("G%d" % i, [128, IQ * 128], BF16) for i in range(2)]
        GTs = [s.sb("GT%d" % i, [128, 128, 256], BF16) for i in range(2)]
        UG = [s.sb("UG%d" % i, [128, WG, 1024], BF16) for i in range(2)]
        VG = [s.sb("VG%d" % i, [128, WG, 1024], BF16) for i in range(2)]
        hbts = [s.sb("hbt%d" % i, [128, DC, 256], BF16) for i in range(2)]
        x1t = s.sb("x1t", [128, DC, 256])
        geb = [s.sb("ge%d" % i, [128, 256], BF16) for i in range(2)]
        ptb = [s.sb("pt%d" % i, [128, 256], BF16) for i in range(2)]
        osb = s.sb("osb", [128, 1024])
        pact = [s.ps("pa%d" % i, [128, 512]) for i in range(2)]
        po = [s.ps("po%d" % i, [128, 512]) for i in range(4)]
        pT = s.ps("pT", [128, 1024], BF16)
        for nm, t, d in (("g2", g2, g2_d), ("identb", identb, identb_d), ("identf", identf, identf_d)):
            s.dma("sp", t[:], d, w=[nm])
        gctr = [0]

        def gbuild(ti):
            c0, n, isc = TILES2[ti]
            GT, gtk = GTs[ti % 2], "GT%d" % (ti % 2)
            for st in range(n // 128):
                t0 = c0 + st * 128
                s.dma("sp", st_[:], sc_d[t0:t0 + 128, :], w=["st"])
                s.dma("sp", tcs[:], tc_d[t0:t0 + 128, :], w=["tcs"])
                s.I("dve", lambda e: e.tensor_scalar(out=negc[:], in0=tcs[:, 8:16], scalar1=-1.0, scalar2=None,
                                                     op0=ALU.mult), r=["tcs"], w=["negc"])
                for q in range(NQ):
                    G, gk = Gb[q % 2], "G%d" % (q % 2)
                    for h in range(8):
                        S, sk = Sb[gctr[0] % 2], "S%d" % (gctr[0] % 2)
                        E, ek = Eb[gctr[0] % 2], "E%d" % (gctr[0] % 2)
                        gctr[0] += 1
                        s.I("pool", lambda e: e.tensor_tensor(
                            out=AP(S, 0, [[128, IQ], [1, 128]]),
                            in0=AP(st_, (2 * h) * 128 + q * IQ, [[1, IQ], [0, 128]]),
                            in1=AP(st_, (2 * h + 1) * 128, [[0, IQ], [1, 128]]), op=ALU.add),
                            r=["st"], w=[sk])
                        s.I("act", lambda e: e.activation(out=E[:], in_=S[:], func=AF.Exp, bias=negc[:, h:h + 1]),
                            r=[sk, "negc"], w=[ek])
                        if h == 0:
                            s.I("dve", lambda e: e.scalar_tensor_tensor(out=G[:], in0=S[:], scalar=tcs[:, h:h + 1],
                                                                        in1=E[:], op0=ALU.is_ge, op1=ALU.mult),
                                r=[sk, ek, "tcs"], w=[gk])
                        else:
                            s.I("dve", lambda e: e.scalar_tensor_tensor(out=Gh[:], in0=S[:], scalar=tcs[:, h:h + 1],
                                                                        in1=E[:], op0=ALU.is_ge, op1=ALU.mult),
                                r=[sk, ek, "tcs"], w=["Gh"])
                            s.I("dve", lambda e: e.tensor_tensor(out=G[:], in0=G[:], in1=Gh[:], op=ALU.add),
                                r=[gk, "Gh"], w=[gk])
                        yield
                    for i0 in range(0, IQ, 8):
                        for il in range(i0, i0 + 8):
                            s.I("pe", lambda e: e.transpose(out=pT[:, (il - i0) * 128:(il - i0 + 1) * 128],
                                                            in_=G[:, il * 128:(il + 1) * 128], identity=identb[:]),
                                r=[gk, "identb"], w=["pT"], sig=(il == i0 + 7))
                        ch0 = q * IQ + i0
                        dst = AP(GT, ch0 * 256 + st * 128, [[256, 8], [1, 128]])
                        src = AP(pT, 0, [[128, 8], [1, 128]])
                        s.I("act", lambda e: e.activation(out=dst, in_=src, func=AF.Copy), r=["pT"], w=[gtk])
                        yield

        npass = len(TILES2)
        for _ in gbuild(0):
            pass
        for ti, (c0, n, isc) in enumerate(TILES2):
            nst = n // 128
            gi = 1 if isc else 0
            GT, gtk = GTs[ti % 2], "GT%d" % (ti % 2)
            hbt, hbk = hbts[ti % 2], "hbt%d" % (ti % 2)
            s.dma("sp", hbt[:, :, :n], h2r[:, :, c0:c0 + n], w=[hbk])
            s.dma("sp", x1t[:, :, :n], x1r[:, :, c0:c0 + n], w=["x1t"])
            nxt = gbuild(ti + 1) if ti + 1 < npass else None
            if nxt is not None:
                nsteps = (TILES2[ti + 1][1] // 128) * NQ * (8 + IQ // 8)
                per = -(-nsteps // 120)
            for c in range(128):
                gsel = (c // WG) % 2
                if c % WG == 0:
                    s.dma("sp", UG[gsel][:], ut_d[c:c + WG].rearrange("c p f -> p c f"), w=["UG%d" % gsel])
                    s.dma("sp", VG[gsel][:], vb_d[c:c + WG].rearrange("c p f -> p c f"), w=["VG%d" % gsel])
                U, V = UG[gsel], VG[gsel]
                pa, pak = pact[c % 2], "pa%d" % (c % 2)
                for k in range(DC):
                    s.I("pe", lambda e: e.matmul(pa[:, :n], lhsT=U[:, c % WG, k * 128:(k + 1) * 128], rhs=hbt[:, k, :n],
                                                 start=(k == 0), stop=(k == DC - 1)),
                        r=["UG%d" % gsel, hbk], w=[pak], sig=(k == DC - 1))
                ge, gek = geb[c % 2], "ge%d" % (c % 2)
                pt, ptk = ptb[c % 2], "pt%d" % (c % 2)
                s.I("act", lambda e: e.activation(out=ge[:, :n], in_=pa[:, :n], func=gelu_func), r=[pak], w=[gek])
                s.I("dve", lambda e: e.tensor_tensor(out=pt[:, :n], in0=ge[:, :n], in1=GT[:, c, :n], op=ALU.mult),
                    r=[gek, gtk], w=[ptk])
                for st in range(nst):
                    for hf in range(2):
                        pidx = st * 2 + hf
                        s.I("pe", lambda e: e.matmul(po[pidx][:], lhsT=pt[:, st * 128:(st + 1) * 128],
                                                     rhs=V[:, c % WG, hf * 512:(hf + 1) * 512],
                                                     start=(c == 0), stop=(c == 127)),
                            r=[ptk, "VG%d" % gsel], w=["po%d" % pidx], sig=(st == nst - 1 and hf == 1))
                if nxt is not None:
                    for _ in range(per):
                        if next(nxt, "done") == "done":
                            nxt = None
                            break
            if nxt is not None:
                for _ in nxt:
                    pass
            for st in range(nst):
                s.I("act", lambda e: e.activation(out=osb[:, 0:512], in_=po[st * 2][:], func=AF.Copy),
                    r=["po%d" % (st * 2)], w=["osb"])
                s.I("dve", lambda e: e.tensor_copy(out=osb[:, 512:1024], in_=po[st * 2 + 1][:]),
                    r=["po%d" % (st * 2 + 1)], w=["osb"])
                for k in range(DC):
                    pf, pfk = pact[k % 2], "pa%d" % (k % 2)
                    s.I("pe", lambda e: e.transpose(out=pf[:, 0:128], in_=osb[:, k * 128:(k + 1) * 128], identity=identf[:]),
                        r=["osb", "identf"], w=[pfk])
                    s.I("dve", lambda e: e.scalar_tensor_tensor(out=x1t[:, k, st * 128:(st + 1) * 128], in0=pf[:, 0:128],
                                                                scalar=g2[:, gi, k:k + 1],
                                                                in1=x1t[:, k, st * 128:(st + 1) * 128],
                                                                op0=ALU.mult, op1=ALU.add),
                        r=[pfk, "g2", "x1t"], w=["x1t"])
            s.dma("sp", x2r[:, :, c0:c0 + n], x1t[:, :, :n], r=["x1t"])
        s.finish()
    return nc


def build_c2q(NQ=16, NB=4):
    nc = new_nc()
    gelu_func = AF.Gelu_apprx_tanh
    x1_d = din(nc, "x1", [D, TOK])
    h2b_d = din(nc, "h2b", [D, TOK], BF16)
    sc_d = din(nc, "sc", [TOK, 2048])
    tc_d = din(nc, "tc", [TOK, 16])
    g2_d = din(nc, "g2", [128, 2, 8])
    ut_d = din(nc, "ut", [128, 128, 1024], BF16)
    vb_d = din(nc, "vb", [128, 128, 1024], BF16)
    identb_d = din(nc, "identb", [128, 128], BF16)
    identf_d = din(nc, "identf", [128, 128])
    x2_d = dout(nc, "x2", [D, TOK])
    x1r = x1_d.rearrange("(k p) t -> p k t", p=128)
    h2r = h2b_d.rearrange("(k p) t -> p k t", p=128)
    x2r = x2_d.rearrange("(k p) t -> p k t", p=128)
    IQ = 128 // NQ
    with ExitStack() as es:
        s = Sched(nc, es)
        g2 = s.sb("g2", [128, 2, 8])
        identb = s.sb("identb", [128, 128], BF16)
        identf = s.sb("identf", [128, 128])
        st_ = s.sb("st", [128, 2048])
        tcs = s.sb("tcs", [128, 16])
        negc = s.sb("negc", [128, 8])
        Sb = [s.sb("S%d" % i, [128, IQ * 128]) for i in range(2)]
        Eb = [s.sb("E%d" % i, [128, IQ * 128], BF16) for i in range(2)]
        Gb = [s.sb("G%d" % i, [128, IQ * 128], BF16) for i in range(3)]
        GTs = [s.sb("GT%d" % i, [128, 128, 256], BF16) for i in range(2)]
        UG = [s.sb("UG%d" % i, [128, 1024], BF16) for i in range(NB)]
        VG = [s.sb("VG%d" % i, [128, 1024], BF16) for i in range(NB)]
        hbts = [s.sb("hbt%d" % i, [128, DC, 256], BF16) for i in range(2)]
        x1t = s.sb("x1t", [128, DC, 256])
        geb = [s.sb("ge%d" % i, [128, 256], BF16) for i in range(2)]
        ptb = [s.sb("pt%d" % i, [128, 256], BF16) for i in range(2)]
        osb = s.sb("osb", [128, 1024])
        pact = [s.ps("pa%d" % i, [128, 512]) for i in range(2)]
        po = [s.ps("po%d" % i, [128, 512]) for i in range(4)]
        pG = s.ps("pG", [128, 1024])
        for nm, t, d in (("g2", g2, g2_d), ("identb", identb, identb_d), ("identf", identf, identf_d)):
            s.dma("sp", t[:], d, w=[nm])
        gctr = [0]

        def gbuild(ti):
            c0, n, isc = TILES2[ti]
            GT, gtk = GTs[ti % 2], "GT%d" % (ti % 2)
            for st in range(n // 128):
                t0 = c0 + st * 128
                s.dma("sp", st_[:], sc_d[t0:t0 + 128, :], w=["st"])
                s.dma("sp", tcs[:], tc_d[t0:t0 + 128, :], w=["tcs"])
                s.I("dve", lambda e: e.tensor_scalar(out=negc[:], in0=tcs[:, 8:16], scalar1=-1.0, scalar2=None,
                                                     op0=ALU.mult), r=["tcs"], w=["negc"])
                assert IQ == 8
                for q in range(NQ):
                    for h in range(8):
                        S, sk = Sb[gctr[0] % 2], "S%d" % (gctr[0] % 2)
                        E, ek = Eb[gctr[0] % 2], "E%d" % (gctr[0] % 2)
                        G, gk = Gb[gctr[0] % 3], "G%d" % (gctr[0] % 3)
                        gctr[0] += 1
                        s.I("pool", lambda e: e.tensor_tensor(
                            out=AP(S, 0, [[128, IQ], [1, 128]]),
                            in0=AP(st_, (2 * h) * 128 + q * IQ, [[1, IQ], [0, 128]]),
                            in1=AP(st_, (2 * h + 1) * 128, [[0, IQ], [1, 128]]), op=ALU.add),
                            r=["st"], w=[sk])
                        s.I("act", lambda e: e.activation(out=E[:], in_=S[:], func=AF.Exp, bias=negc[:, h:h + 1]),
                            r=[sk, "negc"], w=[ek])
                        s.I("dve", lambda e: e.scalar_tensor_tensor(out=G[:], in0=S[:], scalar=tcs[:, h:h + 1],
                                                                    in1=E[:], op0=ALU.is_ge, op1=ALU.mult),
                            r=[sk, ek, "tcs"], w=[gk])
                        for il in range(8):
                            s.I("pe", lambda e: e.matmul(pG[:, il * 128:(il + 1) * 128], lhsT=G[:, il * 128:(il + 1) * 128],
                                                         rhs=identb[:], start=(h == 0 and il % 4 == 0), stop=(h == 7),
                                                         skip_group_check=True),
                                r=[gk, "identb"], w=["pG"], sig=(il == 7))
                        yield
                    ch0 = q * IQ
                    dst = AP(GT, ch0 * 256 + st * 128, [[256, 8], [1, 128]])
                    src = AP(pG, 0, [[128, 8], [1, 128]])
                    s.I("act", lambda e: e.activation(out=dst, in_=src, func=AF.Copy), r=["pG"], w=[gtk])
                    yield

        npass = len(TILES2)
        for _ in gbuild(0):
            pass
        for ti, (c0, n, isc) in enumerate(TILES2):
            nst = n // 128
            gi = 1 if isc else 0
            GT, gtk = GTs[ti % 2], "GT%d" % (ti % 2)
            hbt, hbk = hbts[ti % 2], "hbt%d" % (ti % 2)
            s.dma("sp", hbt[:, :, :n], h2r[:, :, c0:c0 + n], w=[hbk])
            s.dma("sp", x1t[:, :, :n], x1r[:, :, c0:c0 + n], w=["x1t"])
            nxt = gbuild(ti + 1) if ti + 1 < npass else None
            if nxt is not None:
                nsteps = (TILES2[ti + 1][1] // 128) * NQ * 9
                per = -(-nsteps // 120)
            def emit_w(c):
                if c < 128:
                    s.dma("sp", UG[c % NB][:], ut_d[c], w=["UG%d" % (c % NB)])
                    s.dma("sp", VG[c % NB][:], vb_d[c], w=["VG%d" % (c % NB)])

            def emit_u(c):
                pa, pak = pact[c % 2], "pa%d" % (c % 2)
                for k in range(DC):
                    s.I("pe", lambda e: e.matmul(pa[:, :n], lhsT=UG[c % NB][:, k * 128:(k + 1) * 128], rhs=hbt[:, k, :n],
                                                 start=(k == 0), stop=(k == DC - 1)),
                        r=["UG%d" % (c % NB), hbk], w=[pak], sig=(k == DC - 1))
            for cc in range(NB - 1):
                emit_w(cc)
            emit_u(0)
            for c in range(128):
                emit_w(c + NB - 1)
                V = VG[c % NB]
                pa, pak = pact[c % 2], "pa%d" % (c % 2)
                ge, gek = geb[c % 2], "ge%d" % (c % 2)
                pt, ptk = ptb[c % 2], "pt%d" % (c % 2)
                s.I("act", lambda e: e.activation(out=ge[:, :n], in_=pa[:, :n], func=gelu_func), r=[pak], w=[gek])
                if c + 1 < 128:
                    emit_u(c + 1)
                s.I("dve", lambda e: e.tensor_tensor(out=pt[:, :n], in0=ge[:, :n], in1=GT[:, c, :n], op=ALU.mult),
                    r=[gek, gtk], w=[ptk])
                for st in range(nst):
                    for hf in range(2):
                        pidx = st * 2 + hf
                        s.I("pe", lambda e: e.matmul(po[pidx][:], lhsT=pt[:, st * 128:(st + 1) * 128],
                                                     rhs=V[:, hf * 512:(hf + 1) * 512],
                                                     start=(c == 0), stop=(c == 127)),
                            r=[ptk, "VG%d" % (c % NB)], w=["po%d" % pidx], sig=(st == nst - 1 and hf == 1))
                if nxt is not None:
                    for _ in range(per):
                        if next(nxt, "done") == "done":
                            nxt = None
                            break
            if nxt is not None:
                for _ in nxt:
                    pass
            for st in range(nst):
                s.I("act", lambda e: e.activation(out=osb[:, 0:512], in_=po[st * 2][:], func=AF.Copy),
                    r=["po%d" % (st * 2)], w=["osb"])
                s.I("dve", lambda e: e.tensor_copy(out=osb[:, 512:1024], in_=po[st * 2 + 1][:]),
                    r=["po%d" % (st * 2 + 1)], w=["osb"])
                for k in range(DC):
                    pf, pfk = pact[k % 2], "pa%d" % (k % 2)
                    s.I("pe", lambda e: e.transpose(out=pf[:, 0:128], in_=osb[:, k * 128:(k + 1) * 128], identity=identf[:]),
                        r=["osb", "identf"], w=[pfk])
                    s.I("dve", lambda e: e.scalar_tensor_tensor(out=x1t[:, k, st * 128:(st + 1) * 128], in0=pf[:, 0:128],
                                                                scalar=g2[:, gi, k:k + 1],
                                                                in1=x1t[:, k, st * 128:(st + 1) * 128],
                                                                op0=ALU.mult, op1=ALU.add),
                        r=[pfk, "g2", "x1t"], w=["x1t"])
            s.dma("sp", x2r[:, :, c0:c0 + n], x1t[:, :, :n], r=["x1t"])
        s.finish()
    return nc
```
